# Optimizing a Trainium2 kernel written in Bass

```python
import jax
import jax.numpy as jnp
from jax import lax
import numpy as np

D_MODEL = 2048
BATCH = 8
SEQ = 2048
DEPTH = 2

GRID_W = 64
CTX_LEN = 256

NA_HEADS = 8
NA_DIM = 128
NA_KH = 8
NA_KW = 16
DN_HEADS = 4
DN_DIM = 128
DN_CONV = 5
DN_CHUNK = 64
RW_HEADS = 8
RW_DIM = 64
RW_LORA_W = 96
RW_LORA_A = 96
RW_GN_EPS = 64e-5

ROPE_THETA = 10000.0
NORM_EPS = 1e-6
NEG_INF = -1e30

NA_WIDTH = NA_HEADS * NA_DIM
DN_WIDTH = DN_HEADS * DN_DIM
RW_WIDTH = RW_HEADS * RW_DIM
MIX_WIDTH = NA_WIDTH + DN_WIDTH + RW_WIDTH
NA_COLS = 4 * NA_WIDTH
DN_COLS = 4 * DN_WIDTH + 4 * DN_HEADS
RW_COLS = 4 * RW_WIDTH + RW_LORA_W + RW_LORA_A
PROJ_COLS = NA_COLS + DN_COLS + RW_COLS

kernel_name = 'hybrid_na_deltanet_rwkv7_dit_block'


def rms_norm(x, g):
    xf = x.astype(jnp.float32)
    y = xf * lax.rsqrt(jnp.mean(xf * xf, axis=-1, keepdims=True) + NORM_EPS)
    return y * g.astype(jnp.float32)


def l2_norm(x):
    xf = x.astype(jnp.float32)
    return xf * lax.rsqrt(jnp.sum(xf * xf, axis=-1, keepdims=True) + NORM_EPS)


def centred_token_shift(y):
    yp = jnp.pad(y, ((0, 0), (1, 1), (0, 0)))
    return 0.5 * (yp[:, :-2] + yp[:, 2:])


def centred_depthwise_conv(x, w):
    k, ch = w.shape
    return lax.conv_general_dilated(x, w.astype(x.dtype)[:, None, :], window_strides=(1,),
                                    padding=[(k // 2, k // 2)],
                                    dimension_numbers=('NWC', 'WIO', 'NWC'),
                                    feature_group_count=ch)


def axial_rope(x):
    seq_len, d = x.shape[1], x.shape[-1]
    half = d // 2
    nf = half // 2
    t = jnp.arange(seq_len)
    inv_freq = ROPE_THETA ** (-jnp.arange(nf, dtype=jnp.float32) / nf)

    def rotate(xa, pos):
        ang = pos.astype(jnp.float32)[:, None] * inv_freq[None, :]
        cos = jnp.cos(ang)[None, :, None, :]
        sin = jnp.sin(ang)[None, :, None, :]
        x1, x2 = xa[..., :nf], xa[..., nf:]
        return jnp.concatenate([x1 * cos - x2 * sin, x1 * sin + x2 * cos], axis=-1)

    return jnp.concatenate([rotate(x[..., :half], t // GRID_W), rotate(x[..., half:], t % GRID_W)], axis=-1)


def neighbourhood_attention_mixer(p, pc, q_gain, k_gain, rpb, need_ctx):
    H, d, W = NA_HEADS, NA_DIM, NA_WIDTH

    def prepare(t):
        b, l, _ = t.shape
        q = rms_norm(t[..., :W].reshape(b, l, H, d), q_gain) * (d ** -0.5)
        k = rms_norm(t[..., W:2 * W].reshape(b, l, H, d), k_gain)
        v = t[..., 2 * W:3 * W].reshape(b, l, H, d).astype(jnp.float32)
        return q, k, v, t[..., 3 * W:]

    q, k, v, z = prepare(p)
    qc, kc, vc, zc = prepare(pc)
    bsz, seq_len = p.shape[0], p.shape[1]
    ctx_len = pc.shape[1]
    rows = seq_len // GRID_W
    kh = min(NA_KH, rows)
    kw = NA_KW

    def grid(u):
        return u.reshape(bsz, rows, GRID_W, H, d).transpose(0, 3, 1, 2, 4)

    qg, kg, vg = grid(q), grid(k), grid(v)
    r_idx = jnp.arange(rows)
    row_start = jnp.clip(r_idx - kh // 2, 0, rows - kh)
    key_rows = row_start[:, None] + jnp.arange(kh)[None, :]
    k_blk = kg[:, :, key_rows]
    v_blk = vg[:, :, key_rows]
    cols = jnp.arange(GRID_W)
    col_start = jnp.clip(cols - kw // 2, 0, GRID_W - kw)
    col_ok = (cols[None, :] >= col_start[:, None]) & (cols[None, :] < col_start[:, None] + kw)
    row_off = key_rows - r_idx[:, None] + (NA_KH - 1)
    col_off = jnp.clip(cols[None, :] - cols[:, None] + (NA_KW - 1), 0, 2 * NA_KW - 2)
    bias = rpb.astype(jnp.float32)[:, row_off[:, None, :, None], col_off[None, :, None, :]]

    s_loc = jnp.einsum('bhrwd,bhrjud->bhrwju', qg, k_blk) + bias[None]
    s_loc = jnp.where(col_ok[:, None, :], s_loc, NEG_INF)
    kch = kc.transpose(0, 2, 1, 3)
    vch = vc.transpose(0, 2, 1, 3)
    s_ctx = jnp.einsum('bhrwd,bhcd->bhrwc', qg, kch)
    n_loc = kh * GRID_W
    probs = jax.nn.softmax(jnp.concatenate([s_loc.reshape(bsz, H, rows, GRID_W, n_loc), s_ctx], axis=-1), axis=-1)
    p_loc = probs[..., :n_loc].reshape(bsz, H, rows, GRID_W, kh, GRID_W)
    o = (jnp.einsum('bhrwju,bhrjud->bhrwd', p_loc, v_blk)
         + jnp.einsum('bhrwc,bhcd->bhrwd', probs[..., n_loc:], vch))
    o = o.transpose(0, 2, 3, 1, 4).reshape(bsz, seq_len, W)
    y = (o * jax.nn.silu(z.astype(jnp.float32))).astype(p.dtype)

    yc = None
    if need_ctx:
        pcc = jax.nn.softmax(jnp.einsum('bqhd,bkhd->bhqk', qc, kc), axis=-1)
        oc = jnp.einsum('bhqk,bkhd->bqhd', pcc, vc).reshape(bsz, ctx_len, W)
        yc = (oc * jax.nn.silu(zc.astype(jnp.float32))).astype(pc.dtype)
    return y, yc


def chunk_gated_delta_rule(q, k, v, g, beta, s0):
    bsz, nh, seq_len, dk = k.shape
    dv = v.shape[-1]
    cl = DN_CHUNK
    n = seq_len // cl
    q, k, v = (u.reshape(bsz, nh, n, cl, -1) for u in (q, k, v))
    g = jnp.cumsum(g.reshape(bsz, nh, n, cl), axis=-1)
    beta = beta.reshape(bsz, nh, n, cl)
    idx = jnp.arange(cl)
    causal = idx[:, None] >= idx[None, :]
    strict = idx[:, None] > idx[None, :]
    decay = jnp.exp(jnp.where(causal, g[..., :, None] - g[..., None, :], -jnp.inf))
    kb = k * beta[..., None]
    low = jnp.where(strict, jnp.einsum('bhncd,bhnsd->bhncs', kb, k) * decay, 0.0)
    a_mat = low + jnp.eye(cl, dtype=jnp.float32)
    rhs = jnp.concatenate([v * beta[..., None], kb * jnp.exp(g)[..., None]], axis=-1)
    sol = lax.linalg.triangular_solve(a_mat, rhs, left_side=True, lower=True, unit_diagonal=True)
    u_c, w_c = sol[..., :dv], sol[..., dv:]
    attn = jnp.where(causal, jnp.einsum('bhncd,bhnsd->bhncs', q, k) * decay, 0.0)
    q_dec = q * jnp.exp(g)[..., None]
    k_dec = k * jnp.exp(g[..., -1:] - g)[..., None]
    g_last = jnp.exp(g[..., -1])

    def step(s, xs):
        u_i, w_i, q_i, k_i, a_i, gl_i = xs
        v_new = u_i - jnp.einsum('bhcd,bhde->bhce', w_i, s)
        o_i = jnp.einsum('bhcd,bhde->bhce', q_i, s) + jnp.einsum('bhcs,bhse->bhce', a_i, v_new)
        s = s * gl_i[..., None, None] + jnp.einsum('bhcd,bhce->bhde', k_i, v_new)
        return s, o_i

    xs = tuple(jnp.moveaxis(u, 2, 0) for u in (u_c, w_c, q_dec, k_dec, attn, g_last))
    s_fin, o = lax.scan(step, s0, xs)
    o = jnp.moveaxis(o, 0, 2).reshape(bsz, nh, seq_len, dv)
    return o, s_fin


def gated_deltanet_mixer(p, pc, conv_w, a_log, dt_bias, norm_g, need_ctx):
    H, d, W = DN_HEADS, DN_DIM, DN_WIDTH
    a_log = a_log.astype(jnp.float32)
    dt_bias = dt_bias.astype(jnp.float32)

    def prepare(t, with_rope):
        b, l, _ = t.shape
        qkv = jax.nn.silu(centred_depthwise_conv(t[..., :3 * W], conv_w))
        q = l2_norm(qkv[..., :W].reshape(b, l, H, d))
        k = l2_norm(qkv[..., W:2 * W].reshape(b, l, H, d))
        v = qkv[..., 2 * W:].reshape(b, l, H, d).astype(jnp.float32)
        if with_rope:
            q, k = axial_rope(q), axial_rope(k)
        q = q * (d ** -0.5)
        ab = t[..., 4 * W:].astype(jnp.float32)
        a_in = ab[..., :2 * H].reshape(b, l, 2, H)
        b_in = ab[..., 2 * H:].reshape(b, l, 2, H)
        g = -jnp.exp(a_log) * jax.nn.softplus(a_in + dt_bias)
        beta = jax.nn.sigmoid(b_in)
        return (q.transpose(0, 2, 1, 3), k.transpose(0, 2, 1, 3), v.transpose(0, 2, 1, 3),
                g.transpose(2, 0, 3, 1), beta.transpose(2, 0, 3, 1), t[..., 3 * W:4 * W])

    q, k, v, g, beta, z = prepare(p, True)
    qc, kc, vc, gc, betac, zc = prepare(pc, False)
    s0 = jnp.zeros((p.shape[0], H, d, d), jnp.float32)

    def flip(u):
        return jnp.flip(u, axis=2)

    oc_f, sc_f = chunk_gated_delta_rule(qc, kc, vc, gc[0], betac[0], s0)
    o_f, _ = chunk_gated_delta_rule(q, k, v, g[0], beta[0], sc_f)
    oc_b, sc_b = chunk_gated_delta_rule(flip(qc), flip(kc), flip(vc), flip(gc[1]), flip(betac[1]), s0)
    o_b, _ = chunk_gated_delta_rule(flip(q), flip(k), flip(v), flip(g[1]), flip(beta[1]), sc_b)

    def finish(o, zz):
        b, l = zz.shape[0], zz.shape[1]
        on = rms_norm(o.transpose(0, 2, 1, 3), norm_g).reshape(b, l, W)
        return (on * jax.nn.silu(zz.astype(jnp.float32))).astype(zz.dtype)

    y = finish(o_f + flip(o_b), z)
    yc = finish(oc_f + flip(oc_b), zc) if need_ctx else None
    return y, yc


def rwkv7_scan(r, w, k, v, kk, b, s0):
    def step(s, xs):
        r_t, w_t, k_t, v_t, kk_t, b_t = xs
        sa = jnp.einsum('bhvk,bhk->bhv', s, -kk_t)
        s = s * w_t[:, :, None, :] + sa[..., None] * b_t[:, :, None, :] + v_t[..., None] * k_t[:, :, None, :]
        return s, jnp.einsum('bhvk,bhk->bhv', s, r_t)

    xs = tuple(jnp.moveaxis(u, 1, 0) for u in (r, w, k, v, kk, b))
    s_fin, o = lax.scan(step, s0, xs)
    return jnp.moveaxis(o, 0, 1), s_fin


def head_group_norm(o, w, b):
    bsz, l, nh, n = o.shape
    mu = jnp.mean(o, axis=-1, keepdims=True)
    var = jnp.mean(jnp.square(o - mu), axis=-1, keepdims=True)
    y = (o - mu) * lax.rsqrt(var + RW_GN_EPS)
    return y.reshape(bsz, l, nh * n) * w + b


def rwkv7_mixer(p, pc, mu, w0, w_up, a0, a_up, k_k, k_a, r_k, ln_w, ln_b, need_ctx):
    H, N, W = RW_HEADS, RW_DIM, RW_WIDTH
    mu, w0, w_up, a0, a_up, k_k, k_a, r_k, ln_w, ln_b = (
        u.astype(jnp.float32) for u in (mu, w0, w_up, a0, a_up, k_k, k_a, r_k, ln_w, ln_b))

    def heads(u):
        return u.reshape(u.shape[:-1] + (H, N))

    def prepare(t):
        b, l, _ = t.shape
        t = t.astype(jnp.float32)
        t = t + mu * (centred_token_shift(t) - t)
        r, k, v, z = t[..., :W], t[..., W:2 * W], t[..., 2 * W:3 * W], t[..., 3 * W:4 * W]
        wd = t[..., 4 * W:4 * W + RW_LORA_W]
        ad = t[..., 4 * W + RW_LORA_W:]
        w_log = -jax.nn.softplus(-(w0[:, None, None, :] + jnp.einsum('blr,drc->dblc', jnp.tanh(wd), w_up))) - 0.5
        decay = jnp.exp(-jnp.exp(w_log))
        a = jax.nn.sigmoid(a0[:, None, None, :] + jnp.einsum('blr,drc->dblc', ad, a_up))
        kk = l2_norm(heads(k * k_k))
        k_dir = k[None] * (1.0 + (a - 1.0) * k_a)
        return heads(r), heads(k_dir), heads(v), heads(decay), kk, kk[None] * heads(a), z

    r, k, v, w, kk, bb, z = prepare(p)
    rc, kc, vc, wc, kkc, bbc, zc = prepare(pc)
    s0 = jnp.zeros((p.shape[0], H, N, N), jnp.float32)

    def flip(u):
        return jnp.flip(u, axis=1)

    oc_f, sc_f = rwkv7_scan(rc, wc[0], kc[0], vc, kkc, bbc[0], s0)
    o_f, _ = rwkv7_scan(r, w[0], k[0], v, kk, bb[0], sc_f)
    oc_b, sc_b = rwkv7_scan(flip(rc), flip(wc[1]), flip(kc[1]), flip(vc), flip(kkc), flip(bbc[1]), s0)
    o_b, _ = rwkv7_scan(flip(r), flip(w[1]), flip(k[1]), flip(v), flip(kk), flip(bb[1]), sc_b)

    def readout(o, rr, kd, vv):
        bonus = jnp.sum(rr * kd * r_k, axis=-1, keepdims=True) * vv
        gn = head_group_norm(o, ln_w, ln_b)
        return gn + bonus.reshape(gn.shape)

    def finish(of, ob, rr, kd, vv, zz):
        y = readout(of, rr, kd[0], vv) + readout(ob, rr, kd[1], vv)
        return y * jax.nn.silu(zz)

    y = finish(o_f, flip(o_b), r, k, v, z).astype(p.dtype)
    yc = finish(oc_f, flip(oc_b), rc, kc, vc, zc).astype(pc.dtype) if need_ctx else None
    return y, yc


def hybrid_layer(x, ctx, c, c_ctx, w_mod, b_mod, norm_g, w_in, w_out, na_qn, na_kn, na_rpb,
                 dn_conv, dn_A_log, dn_dt_bias, dn_norm, rw_mu, rw_w0, rw_w_up, rw_a0, rw_a_up,
                 rw_k_k, rw_k_a, rw_r_k, rw_ln_w, rw_ln_b, need_ctx):
    mod = jax.nn.silu(c) @ w_mod + b_mod
    mod_c = jax.nn.silu(c_ctx) @ w_mod + b_mod
    shift, scale, gate = jnp.split(mod, 3, axis=-1)
    shift_c, scale_c, gate_c = jnp.split(mod_c, 3, axis=-1)
    h = (rms_norm(x, norm_g) * (1.0 + scale[:, None, :]) + shift[:, None, :]).astype(x.dtype)
    hc = (rms_norm(ctx, norm_g) * (1.0 + scale_c) + shift_c).astype(ctx.dtype)
    p = h @ w_in
    pc = hc @ w_in
    b0, b1 = NA_COLS, NA_COLS + DN_COLS
    ya, yac = neighbourhood_attention_mixer(p[..., :b0], pc[..., :b0], na_qn, na_kn, na_rpb, need_ctx)
    yb, ybc = gated_deltanet_mixer(p[..., b0:b1], pc[..., b0:b1], dn_conv, dn_A_log, dn_dt_bias, dn_norm, need_ctx)
    yr, yrc = rwkv7_mixer(p[..., b1:], pc[..., b1:], rw_mu, rw_w0, rw_w_up, rw_a0, rw_a_up,
                          rw_k_k, rw_k_a, rw_r_k, rw_ln_w, rw_ln_b, need_ctx)
    x = x + gate[:, None, :] * (jnp.concatenate([ya, yb, yr], axis=-1) @ w_out)
    if need_ctx:
        ctx = ctx + gate_c * (jnp.concatenate([yac, ybc, yrc], axis=-1) @ w_out)
    return x, ctx


def setup_inputs(seed: int = 0) -> dict:
    key = jax.random.key(seed)
    ks = jax.random.split(key, 26)

    def nrm(k, shape, s):
        return s * jax.random.normal(k, shape, jnp.float32)

    dt = jnp.exp(jax.random.uniform(ks[14], (DEPTH, 2, DN_HEADS), jnp.float32, -6.9078, -2.3026))
    return {
        'x': nrm(ks[0], (BATCH, SEQ, D_MODEL), 1.0),
        'c': nrm(ks[1], (BATCH, D_MODEL), 1.0),
        'ctx': nrm(ks[2], (BATCH, CTX_LEN, D_MODEL), 1.0),
        'c_ctx': nrm(ks[3], (D_MODEL,), 1.0),
        'w_mod': nrm(ks[4], (DEPTH, D_MODEL, 3 * D_MODEL), 0.5 * D_MODEL ** -0.5),
        'b_mod': nrm(ks[5], (DEPTH, 3 * D_MODEL), 0.01),
        'norm_g': 1.0 + nrm(ks[6], (DEPTH, D_MODEL), 0.02),
        'w_in': nrm(ks[7], (DEPTH, D_MODEL, PROJ_COLS), D_MODEL ** -0.5),
        'w_out': nrm(ks[8], (DEPTH, MIX_WIDTH, D_MODEL), MIX_WIDTH ** -0.5),
        'na_qn': 1.0 + nrm(ks[9], (DEPTH, NA_DIM), 0.02),
        'na_kn': 1.0 + nrm(ks[10], (DEPTH, NA_DIM), 0.02),
        'na_rpb': nrm(ks[11], (DEPTH, NA_HEADS, 2 * NA_KH - 1, 2 * NA_KW - 1), 0.1),
        'dn_conv': nrm(ks[12], (DEPTH, DN_CONV, 3 * DN_WIDTH), DN_CONV ** -0.5),
        'dn_A_log': jnp.log(jax.random.uniform(ks[13], (DEPTH, 2, DN_HEADS), jnp.float32, 1.0, 16.0)),
        'dn_dt_bias': dt + jnp.log(-jnp.expm1(-dt)),
        'dn_norm': 1.0 + nrm(ks[15], (DEPTH, DN_DIM), 0.02),
        'rw_mu': 0.5 + nrm(ks[16], (DEPTH, RW_COLS), 0.1),
        'rw_w0': jax.random.uniform(ks[17], (DEPTH, 2, RW_WIDTH), jnp.float32, -6.0, -1.0),
        'rw_w_up': nrm(ks[18], (DEPTH, 2, RW_LORA_W, RW_WIDTH), 0.1),
        'rw_a0': nrm(ks[19], (DEPTH, 2, RW_WIDTH), 0.1),
        'rw_a_up': nrm(ks[20], (DEPTH, 2, RW_LORA_A, RW_WIDTH), 0.1),
        'rw_k_k': 0.85 + nrm(ks[21], (DEPTH, RW_WIDTH), 0.05),
        'rw_k_a': 1.0 + nrm(ks[22], (DEPTH, RW_WIDTH), 0.05),
        'rw_r_k': nrm(ks[23], (DEPTH, RW_HEADS, RW_DIM), 0.1),
        'rw_ln_w': 1.0 + nrm(ks[24], (DEPTH, RW_WIDTH), 0.02),
        'rw_ln_b': nrm(ks[25], (DEPTH, RW_WIDTH), 0.01),
    }


def reference(x, c, ctx, c_ctx, w_mod, b_mod, norm_g, w_in, w_out, na_qn, na_kn, na_rpb,
              dn_conv, dn_A_log, dn_dt_bias, dn_norm, rw_mu, rw_w0, rw_w_up, rw_a0, rw_a_up,
              rw_k_k, rw_k_a, rw_r_k, rw_ln_w, rw_ln_b):
    for layer in range(DEPTH):
        x, ctx = hybrid_layer(
            x, ctx, c, c_ctx, w_mod[layer], b_mod[layer], norm_g[layer], w_in[layer], w_out[layer],
            na_qn[layer], na_kn[layer], na_rpb[layer], dn_conv[layer], dn_A_log[layer],
            dn_dt_bias[layer], dn_norm[layer], rw_mu[layer], rw_w0[layer], rw_w_up[layer],
            rw_a0[layer], rw_a_up[layer], rw_k_k[layer], rw_k_a[layer], rw_r_k[layer],
            rw_ln_w[layer], rw_ln_b[layer], need_ctx=layer < DEPTH - 1)
    return x
```

```python
import numpy as np
import ml_dtypes
import concourse.bass as bass
import concourse.mybir as mybir
from concourse.bass_utils import run_bass_kernel_spmd

F32 = mybir.dt.float32
BF16 = mybir.dt.bfloat16
ALU = mybir.AluOpType
AF = mybir.ActivationFunctionType
AX = mybir.AxisListType

NA_H, NA_D = 8, 128
DN_H, DN_D = 4, 128
RW_H, RW_N = 8, 64
GRID_W = 64
PROJ = 8400
MIX = 2048
NORM_EPS = 1e-6
RW_GN_EPS = 64e-5
BIG = 1e30


class Res:
    __slots__ = ("name", "w", "rs")

    def __init__(self, name=""):
        self.name = name
        self.w = None
        self.rs = {}


class TT:
    def __init__(self, t, name=""):
        self.t = t
        self.r = Res(name)

    def __getitem__(self, k):
        return self.t[k]


def _res(xs):
    out = []
    for x in xs:
        if x is None:
            continue
        out.append(x.r if isinstance(x, TT) else x)
    return out


class Sched:
    ENGS = ("pe", "act", "dve", "pool", "sp")
    DMAQ = ("sp", "pool", "act")

    def __init__(self, nc, n_dma_sems=14):
        self.nc = nc
        self.ops = {e: [] for e in self.ENGS}
        self.cnt = {e: 0 for e in self.ENGS}
        self.known = {e: {} for e in self.ENGS}
        self.n_dma_sems = n_dma_sems
        self.dma_cnt = {}
        self.dma_rr = {q: 0 for q in self.DMAQ}
        self.sems = {}
        self.stack = []
        self.uid = 0

    def enter(self, cm):
        v = cm.__enter__()
        self.stack.append(cm)
        return v

    def sb(self, name, shape, dtype=F32):
        self.uid += 1
        return TT(self.enter(self.nc.sbuf_tensor("%s_%d" % (name, self.uid), list(shape), dtype)), name)

    def ps(self, name, shape, dtype=F32):
        self.uid += 1
        return TT(self.enter(self.nc.psum_tensor("%s_%d" % (name, self.uid), list(shape), dtype)), name)

    def mark(self):
        return len(self.stack)

    def release(self, mark):
        while len(self.stack) > mark:
            self.stack.pop().__exit__(None, None, None)

    def _record(self, eng, fn, reads, writes, token, extra=()):
        deps = {}

        def need(k, v):
            if deps.get(k, 0) < v:
                deps[k] = v

        for r in reads:
            if r.w is not None:
                need(*r.w)
        for w in writes:
            if w.w is not None:
                need(*w.w)
            for k, v in w.rs.items():
                need(k, v)
        for k, v in extra:
            need(k, v)
        waits = []
        kn = self.known[eng]
        for k, v in deps.items():
            if k == eng and eng == "pe":
                continue
            if kn.get(k, 0) >= v:
                continue
            kn[k] = v
            waits.append((k, v))
        self.ops[eng].append((waits, fn, token))
        k, v = token
        for r in reads:
            if r.rs.get(k, 0) < v:
                r.rs[k] = v
        for w in writes:
            w.w = token
            w.rs = {}

    def op(self, eng, fn, R=(), W=()):
        self.cnt[eng] += 1
        token = (eng, self.cnt[eng])
        self._record(eng, fn, _res(R), _res(W), token)

    def dma(self, q, out, in_, R=(), W=(), slow=False):
        i = self.dma_rr[q]
        self.dma_rr[q] = (i + 1) % self.n_dma_sems
        key = "d_%s_%d" % (q, i)
        prev = self.dma_cnt.get(key, 0)
        self.dma_cnt[key] = prev + 16
        token = (key, prev + 16)
        extra = [(key, prev)] if prev > 0 else []
        if slow:
            fn = lambda e: e.dma_start(out=out, in_=in_, allow_slow_non_contiguous=True)
        else:
            fn = lambda e: e.dma_start(out=out, in_=in_)
        self._record(q, fn, _res(R), _res(W), token, extra)

    def barrier(self):
        cur = {}
        for e in self.ENGS:
            if self.cnt[e] > 0:
                cur[e] = self.cnt[e]
        for k, v in self.dma_cnt.items():
            cur[k] = v
        for e in self.ENGS:
            waits = []
            kn = self.known[e]
            for k, v in cur.items():
                if kn.get(k, 0) >= v:
                    continue
                kn[k] = v
                waits.append((k, v))
            if waits:
                self.ops[e].append((waits, None, None))

    def mm(self, out, lhsT, rhs, start=True, stop=True, R=(), W=()):
        self.op("pe", lambda e: e.matmul(out, lhsT=lhsT, rhs=rhs, start=start, stop=stop), R, W)

    def tr(self, out, in_, ident, R=(), W=()):
        self.op("pe", lambda e: e.transpose(out=out, in_=in_, identity=ident), R, W)

    def act(self, out, in_, func, R=(), W=(), **kw):
        self.op("act", lambda e: e.activation(out=out, in_=in_, func=func, **kw), R, W)

    def tt(self, eng, out, in0, in1, op, R=(), W=()):
        self.op(eng, lambda e: e.tensor_tensor(out=out, in0=in0, in1=in1, op=op), R, W)

    def ts(self, eng, out, in0, s1, s2, op0, op1=None, R=(), W=()):
        if op1 is None:
            self.op(eng, lambda e: e.tensor_scalar(out=out, in0=in0, scalar1=s1, scalar2=None, op0=op0), R, W)
        else:
            self.op(eng, lambda e: e.tensor_scalar(out=out, in0=in0, scalar1=s1, scalar2=s2, op0=op0, op1=op1), R, W)

    def stt(self, eng, out, in0, scalar, in1, op0, op1, R=(), W=()):
        self.op(eng, lambda e: e.scalar_tensor_tensor(out=out, in0=in0, scalar=scalar, in1=in1, op0=op0, op1=op1), R, W)

    def copy(self, eng, out, in_, R=(), W=()):
        if eng == "act":
            self.op("act", lambda e: e.activation(out=out, in_=in_, func=AF.Copy), R, W)
        else:
            self.op(eng, lambda e: e.tensor_copy(out=out, in_=in_), R, W)

    def memset(self, eng, out, val, W=()):
        self.op(eng, lambda e: e.memset(out, val), (), W)

    def recip(self, out, in_, R=(), W=()):
        self.op("dve", lambda e: e.reciprocal(out=out, in_=in_), R, W)

    def emit(self):
        nc = self.nc
        keys = list(self.ENGS) + sorted(self.dma_cnt.keys())
        for k in keys:
            self.sems[k] = self.enter(nc.semaphore("s_" + k))
        block = self.enter(nc.Block())
        sems = self.sems

        def run(engname):
            def body(eng):
                for waits, fn, token in self.ops[engname]:
                    for k, v in waits:
                        eng.wait_ge(sems[k], v)
                    if fn is None:
                        continue
                    ins = fn(eng)
                    k, v = token
                    ins.then_inc(sems[k], 16 if k.startswith("d_") else 1)
            return body

        block.tensor(run("pe"))
        block.scalar(run("act"))
        block.vector(run("dve"))
        block.gpsimd(run("pool"))
        block.sync(run("sp"))
        while self.stack:
            self.stack.pop().__exit__(None, None, None)


def interleave(gens):
    gens = list(gens)
    while gens:
        for g in list(gens):
            try:
                next(g)
            except StopIteration:
                gens.remove(g)


def build_consts():
    cols = {}
    parts = []

    def add(name, arr):
        arr = np.asarray(arr, np.float32)
        assert arr.shape[0] == 128
        cols[name] = (sum(p.shape[1] for p in parts), arr.shape[1])
        parts.append(arr)

    i = np.arange(128)[:, None]
    j = np.arange(128)[None, :]
    same = (i // 64) == (j // 64)
    add("ident", (i == j))
    add("ones", np.ones((128, 128)))
    add("tri_f", same & (i <= j))
    add("tri_b", same & (i >= j))
    add("blk", same)
    add("ind0", np.broadcast_to(i < 64, (128, 128)))
    add("ind1", np.broadcast_to(i >= 64, (128, 128)))
    add("ind2", np.concatenate([(i < 64), (i >= 64)], axis=1))
    add("LS", same & (i > j))
    add("US", same & (i < j))
    add("LI", same & (i >= j))
    add("UI", same & (i <= j))
    add("PM_LI", np.where(same & (i >= j), 0.0, BIG))
    add("PM_UI", np.where(same & (i <= j), 0.0, BIG))
    add("NM_LI", np.where(same & (i >= j), 0.0, -BIG))
    add("NM_UI", np.where(same & (i <= j), 0.0, -BIG))
    add("OFFD", (i != j))
    Rt = np.zeros((128, 128), np.float32)
    for base in (0, 64):
        for m in range(32):
            Rt[base + m + 32, base + m] = -1.0
            Rt[base + m, base + m + 32] = 1.0
    add("Rt", Rt)
    wq = np.arange(64)
    cs = np.clip(wq - 8, 0, 48)
    wk = np.arange(64)
    ok = (wk[:, None] >= cs[None, :]) & (wk[:, None] < cs[None, :] + 16)
    mneg = np.where(ok, 0.0, -BIG).astype(np.float32)
    add("na_mask", np.concatenate([mneg, mneg], axis=0))
    return np.concatenate(parts, axis=1), cols


def build_rope(SEQ):
    nf = 32
    inv = 10000.0 ** (-np.arange(nf, dtype=np.float32) / nf)
    t = np.arange(SEQ)
    pos_r = (t // GRID_W).astype(np.float32)
    pos_c = (t % GRID_W).astype(np.float32)
    ang = np.zeros((128, SEQ), np.float32)
    for d in range(128):
        pos = pos_r if d < 64 else pos_c
        ang[d] = pos * inv[d % 32]
    return np.stack([np.cos(ang), np.sin(ang)], axis=1).astype(np.float32)


class Builder:
    def __init__(self, D=2048, SEQ=2048, CTX=256, DEPTH=2, debug=False, phases=None):
        self.D, self.SEQ, self.CTX, self.DEPTH = D, SEQ, CTX, DEPTH
        self.T = SEQ + CTX
        self.NT = self.T // 128
        self.NTC = CTX // 128
        self.KT = D // 128
        self.ROWS = SEQ // GRID_W
        self.debug = debug
        self.phases = phases
        self.consts_np, self.ccols = build_consts()
        self.rope_np = build_rope(SEQ)

    def on(self, ph):
        return self.phases is None or ph in self.phases

    def declare(self):
        nc = self.nc
        D, SEQ, CTX, L = self.D, self.SEQ, self.CTX, self.DEPTH

        def inp(name, shape):
            return nc.dram_tensor(name, list(shape), F32, kind="ExternalInput").ap()

        self.x_in = inp("x", [SEQ, D])
        self.ctx_in = inp("ctx", [CTX, D])
        self.cc_in = inp("cc", [2, D])
        self.w_mod = inp("w_mod", [L, D, 3 * D])
        self.b_mod = inp("b_mod", [L, 3 * D])
        self.norm_g = inp("norm_g", [L, D])
        self.w_in = inp("w_in", [L, D, PROJ])
        self.w_out = inp("w_out", [L, MIX, D])
        self.na_qn = inp("na_qn", [L, 128])
        self.na_kn = inp("na_kn", [L, 128])
        self.rpb = inp("rpb_exp", [L, NA_H, 15 * 64, 64])
        self.dn_conv = inp("dn_conv", [L, 5, 1536])
        self.dn_A = inp("dn_A_log", [L, 8])
        self.dn_dt = inp("dn_dt_bias", [L, 8])
        self.dn_norm = inp("dn_norm", [L, 128])
        self.rw_mu = inp("rw_mu", [L, 2240])
        self.rw_w0 = inp("rw_w0", [L, 2, 512])
        self.rw_w_up = inp("rw_w_up", [L, 2, 96, 512])
        self.rw_a0 = inp("rw_a0", [L, 2, 512])
        self.rw_a_up = inp("rw_a_up", [L, 2, 96, 512])
        self.rw_k_k = inp("rw_k_k", [L, 512])
        self.rw_k_a = inp("rw_k_a", [L, 512])
        self.rw_r_k = inp("rw_r_k", [L, 512])
        self.rw_ln_w = inp("rw_ln_w", [L, 512])
        self.rw_ln_b = inp("rw_ln_b", [L, 512])
        self.consts_in = inp("consts", list(self.consts_np.shape))
        self.rope_in = inp("rope", [128, 2, SEQ])
        self.out = nc.dram_tensor("out", [SEQ, D], F32, kind="ExternalOutput").ap()

        dk = "ExternalOutput" if self.debug else "Internal"
        T = self.T
        self.mod_d = nc.dram_tensor("mod_d", [L, 2, 3 * D], F32, kind=dk).ap()
        self.pT = nc.dram_tensor("pT", [40 * 128, T], F32, kind=dk).ap()
        self.pk = nc.dram_tensor("pk", [T, 3280], F32, kind=dk).ap()
        self.yT = nc.dram_tensor("yT", [MIX, T], BF16, kind=dk).ap()
        self.xs = [nc.dram_tensor("xs%d" % i, [T, D], F32, kind=dk).ap() for i in range(2)]
        self.r_mod = Res("mod_d")
        self.r_pT = [Res("pT%d" % i) for i in range(40)]
        self.r_pk = [Res("pk%d" % i) for i in range(self.NT)]
        self.r_yT = [Res("yT%d" % i) for i in range(16)]
        self.r_xs = [[Res("xs%d_%d" % (i, t)) for t in range(self.NT)] for i in range(2)]

    def C(self, name, rows=slice(0, 128), c0=0, n=None):
        o, w = self.ccols[name]
        if n is None:
            n = w - c0
        return self.consts[rows, o + c0:o + c0 + n]

    def CB(self, name, rows=slice(0, 128), c0=0, n=None):
        o, w = self.ccols[name]
        if n is None:
            n = w - c0
        return self.consts_bf[rows, o + c0:o + c0 + n]

    def build(self):
        self.nc = nc = bass.Bass("TRN2", target_bir_lowering=False)
        self.declare()
        self.S = S = Sched(nc)
        NCON = self.consts_np.shape[1]
        self.consts = S.sb("consts", [128, NCON], F32)
        self.consts_bf = S.sb("consts_bf", [128, NCON], BF16)
        S.dma("sp", self.consts[:, :], self.consts_in, W=[self.consts])
        S.copy("dve", self.consts_bf[:, :], self.consts[:, :], R=[self.consts], W=[self.consts_bf])
        self.PB = [S.ps("pb%d" % i, [128, 512], F32) for i in range(8)]
        S.barrier()
        for l in range(self.DEPTH):
            if self.on("mod"):
                self.phase_mod(l)
        for l in range(self.DEPTH):
            last = l == self.DEPTH - 1
            if self.on("proj"):
                self.phase_norm_proj(l)
            if self.on("na"):
                self.phase_na(l, need_ctx=not last)
            if self.on("dn"):
                self.phase_dn(l)
            if self.on("rw"):
                self.phase_rw(l)
            if self.on("out"):
                self.phase_out(l, last)
        S.barrier()
        S.emit()
        return nc

    def src_rows(self, l, tt):
        if l == 0:
            if tt < self.NTC:
                return self.ctx_in[tt * 128:(tt + 1) * 128, :], None
            t0 = (tt - self.NTC) * 128
            return self.x_in[t0:t0 + 128, :], None
        b = (l - 1) % 2
        return self.xs[b][tt * 128:(tt + 1) * 128, :], self.r_xs[b][tt]

    def phase_mod(self, l):
        S, D, KT = self.S, self.D, self.KT
        mk = S.mark()
        cT = S.sb("cT", [128, 2, KT])
        for m in range(2):
            S.dma("sp", cT[:, m, :], self.cc_in[m].rearrange("(kt p) -> p kt", p=128), W=[cT], slow=True)
        S.act(cT[:, :, :], cT[:, :, :], AF.Silu, R=[cT], W=[cT])
        bm = S.sb("bm", [2, 3 * D])
        S.dma("sp", bm[:, :], self.b_mod[l].partition_broadcast(2), W=[bm])
        msb = S.sb("msb", [2, 3 * D])
        wm = [S.sb("wm", [128, KT, 512]) for _ in range(2)]
        wv = self.w_mod[l].rearrange("(kt p) n -> p kt n", p=128)
        for g, g0 in enumerate(range(0, 3 * D, 512)):
            n = min(512, 3 * D - g0)
            w = wm[g % 2]
            S.dma("sp", w[:, :, 0:n], wv[:, :, g0:g0 + n], W=[w])
            ps = self.PB[g % 2]
            for kt in range(KT):
                S.mm(ps[0:2, 0:n], cT[:, :, kt], w[:, kt, 0:n], start=(kt == 0), stop=(kt == KT - 1), R=[cT, w], W=[ps])
            S.tt("dve", msb[:, g0:g0 + n], ps[0:2, 0:n], bm[:, g0:g0 + n], ALU.add, R=[ps, bm], W=[msb])
        S.dma("sp", self.mod_d[l], msb[:, :], R=[msb], W=[self.r_mod])
        S.barrier()
        S.release(mk)

    def fm_cols(self):
        return list(range(0, 2048, 128)) + list(range(3072, 4096, 128)) + list(range(4096, 6144, 128))

    def tm_groups(self):
        g = []
        for c0 in range(2048, 3072, 256):
            g.append((c0, 256, c0 - 2048))
        g.append((6144, 16, 1024))
        for c0 in range(6160, 8400, 256):
            g.append((c0, min(256, 8400 - c0), 1040 + c0 - 6160))
        return g

    def phase_norm_proj(self, l):
        S, D, KT, NT, T = self.S, self.D, self.KT, self.NT, self.T
        mk = S.mark()
        hT = S.sb("hT", [128, KT, T], BF16)
        r_hT = [Res("hT%d" % t) for t in range(NT)]
        mk2 = S.mark()
        AB = {}
        gbc = S.sb("gbc", [128, D])
        S.dma("sp", gbc[:, :], self.norm_g[l].partition_broadcast(128), W=[gbc])
        for which, row in (("lat", 0), ("ctx", 1)):
            A = S.sb("A" + which, [128, D])
            B = S.sb("B" + which, [128, D])
            S.dma("sp", A[:, :], self.mod_d[l, row, D:2 * D].partition_broadcast(128), R=[self.r_mod], W=[A])
            S.dma("sp", B[:, :], self.mod_d[l, row, 0:D].partition_broadcast(128), R=[self.r_mod], W=[B])
            S.stt("dve", A[:, :], A[:, :], 1.0, gbc[:, :], ALU.add, ALU.mult, R=[A, gbc], W=[A])
            AB[which] = (A, B)
        xt = [S.sb("xt", [128, D]) for _ in range(2)]
        tmp = S.sb("tmp", [128, D])
        hb = [S.sb("hb", [128, D], BF16) for _ in range(2)]
        ss = [S.sb("ss", [128, 1]) for _ in range(2)]
        for tt in range(NT):
            src, rsrc = self.src_rows(l, tt)
            x_ = xt[tt % 2]
            h_ = hb[tt % 2]
            s_ = ss[tt % 2]
            A, B = AB["ctx" if tt < self.NTC else "lat"]
            S.dma("sp", x_[:, :], src, R=[rsrc], W=[x_])
            S.memset("pool", s_[:, :], 0.0, W=[s_])
            S.act(tmp[:, :], x_[:, :], AF.Square, R=[x_, s_], W=[tmp, s_], accum_out=s_[:, :])
            S.act(s_[:, :], s_[:, :], AF.Sqrt, R=[s_], W=[s_], bias=NORM_EPS, scale=1.0 / D)
            S.recip(s_[:, :], s_[:, :], R=[s_], W=[s_])
            S.stt("dve", tmp[:, :], x_[:, :], s_[:, 0:1], A[:, :], ALU.mult, ALU.mult, R=[x_, s_, A], W=[tmp])
            S.tt("dve", h_[:, :], tmp[:, :], B[:, :], ALU.add, R=[tmp, B], W=[h_])
            for k0 in range(0, KT, 8):
                nk = min(8, KT - k0)
                pb = self.PB[(tt * 2 + k0 // 8) % 4]
                pv = pb[:, :].bitcast(BF16)
                for k in range(nk):
                    S.tr(pv[:, k * 128:(k + 1) * 128], h_[:, (k0 + k) * 128:(k0 + k + 1) * 128], self.CB("ident"), R=[h_, self.consts_bf], W=[pb])
                S.copy("act" if (k0 // 8) % 2 == 0 else "dve", hT[:, k0:k0 + nk, tt * 128:(tt + 1) * 128],
                       pv[:, 0:nk * 128].rearrange("p (k t) -> p k t", t=128), R=[pb], W=[r_hT[tt]])
        S.barrier()
        S.release(mk2)
        wv = self.w_in[l].rearrange("(kt p) n -> p kt n", p=128)
        wf = [S.sb("wf", [128, KT, 256]) for _ in range(2)]
        wb = [S.sb("wb", [128, KT, 256], BF16) for _ in range(2)]
        stage = [S.sb("stage", [128, T]) for _ in range(2)]
        fm = self.fm_cols()
        silu_tiles = set(range(16, 24)) | set(range(36, 40))
        nblk = (T + 511) // 512
        gi = 0
        cnt = 0
        for g0 in range(0, len(fm), 2):
            c0 = fm[g0]
            assert fm[g0 + 1] == c0 + 128
            wf_, wb_ = wf[gi % 2], wb[gi % 2]
            gi += 1
            S.dma("sp", wf_[:, :, :], wv[:, :, c0:c0 + 256], W=[wf_])
            S.copy("pool", wb_[:, :, :], wf_[:, :, :], R=[wf_], W=[wb_])
            for ct in range(2):
                fi = g0 + ct
                st = stage[fi % 2]
                for b in range(nblk):
                    t0 = b * 512
                    n = min(512, T - t0)
                    pb = self.PB[cnt % 4 + 4]
                    cnt += 1
                    rr = [r_hT[t] for t in range(t0 // 128, (t0 + n) // 128)]
                    for kt in range(KT):
                        S.mm(pb[:, 0:n], wb_[:, kt, ct * 128:(ct + 1) * 128], hT[:, kt, t0:t0 + n], start=(kt == 0), stop=(kt == KT - 1),
                             R=[wb_] + rr, W=[pb])
                    if fi in silu_tiles:
                        S.act(st[:, t0:t0 + n], pb[:, 0:n], AF.Silu, R=[pb], W=[st])
                    elif cnt % 2 == 0:
                        S.copy("act", st[:, t0:t0 + n], pb[:, 0:n], R=[pb], W=[st])
                    else:
                        S.copy("dve", st[:, t0:t0 + n], pb[:, 0:n], R=[pb], W=[st])
                S.dma("sp", self.pT[fi * 128:(fi + 1) * 128, :], st[:, :], R=[st], W=[self.r_pT[fi]])
        stg = [S.sb("stg", [128, 256]) for _ in range(3)]
        for (c0, n, pc0) in self.tm_groups():
            wf_, wb_ = wf[gi % 2], wb[gi % 2]
            gi += 1
            S.dma("sp", wf_[:, :, 0:n], wv[:, :, c0:c0 + n], W=[wf_])
            S.copy("pool", wb_[:, :, 0:n], wf_[:, :, 0:n], R=[wf_], W=[wb_])
            for tt in range(NT):
                pb = self.PB[cnt % 4 + 4]
                sg = stg[cnt % 3]
                cnt += 1
                for kt in range(KT):
                    S.mm(pb[:, 0:n], hT[:, kt, tt * 128:(tt + 1) * 128], wb_[:, kt, 0:n], start=(kt == 0), stop=(kt == KT - 1),
                         R=[wb_, r_hT[tt]], W=[pb])
                S.copy("act" if cnt % 2 == 0 else "dve", sg[:, 0:n], pb[:, 0:n], R=[pb], W=[sg])
                S.dma("sp", self.pk[tt * 128:(tt + 1) * 128, pc0:pc0 + n], sg[:, 0:n], R=[sg], W=[self.r_pk[tt]])
        S.barrier()
        S.release(mk)

    def phase_out(self, l, last):
        S, D, NT, T = self.S, self.D, self.NT, self.T
        mk = S.mark()
        MT = MIX // 128
        wo = S.sb("wo", [128, MT, D], BF16)
        wst = [S.sb("wst", [128, D]) for _ in range(2)]
        wv = self.w_out[l].rearrange("(mt p) n -> p mt n", p=128)
        for i in range(MT):
            w_ = wst[i % 2]
            S.dma("sp", w_[:, :], wv[:, i, :], W=[w_])
            S.copy("pool", wo[:, i, :], w_[:, :], R=[w_], W=[wo])
        TB = 512
        ysb = [S.sb("ysb", [128, MT, TB], BF16) for _ in range(2)]
        yv = self.yT.rearrange("(mt p) t -> p mt t", p=128)
        G = {}
        for which, row in (("lat", 0), ("ctx", 1)):
            g_ = S.sb("G" + which, [128, D])
            S.dma("sp", g_[:, :], self.mod_d[l, row, 2 * D:3 * D].partition_broadcast(128), R=[self.r_mod], W=[g_])
            G[which] = g_
        xt = [S.sb("xt", [128, D]) for _ in range(2)]
        xo = [S.sb("xo", [128, D]) for _ in range(2)]
        cnt = 0
        cur_blk = -1
        for tt in range(NT):
            if last and tt < self.NTC:
                continue
            blk = (tt * 128) // TB
            if blk != cur_blk:
                cur_blk = blk
                yb_ = ysb[blk % 2]
                nb = min(TB, T - blk * TB)
                S.dma("sp", yb_[:, :, 0:nb], yv[:, :, blk * TB:blk * TB + nb], R=self.r_yT, W=[yb_])
            yo = tt * 128 - blk * TB
            src, rsrc = self.src_rows(l, tt)
            x_, o_ = xt[tt % 2], xo[tt % 2]
            g_ = G["ctx" if tt < self.NTC else "lat"]
            S.dma("sp", x_[:, :], src, R=[rsrc], W=[x_])
            for c0 in range(0, D, 512):
                n = min(512, D - c0)
                pb = self.PB[cnt % 4]
                cnt += 1
                for mt in range(MT):
                    S.mm(pb[:, 0:n], yb_[:, mt, yo:yo + 128], wo[:, mt, c0:c0 + n], start=(mt == 0), stop=(mt == MT - 1),
                         R=[yb_, wo], W=[pb])
                S.tt("dve", o_[:, c0:c0 + n], pb[:, 0:n], g_[:, c0:c0 + n], ALU.mult, R=[pb, g_], W=[o_])
            S.tt("pool", o_[:, :], o_[:, :], x_[:, :], ALU.add, R=[o_, x_], W=[o_])
            if last:
                t0 = (tt - self.NTC) * 128
                S.dma("sp", self.out[t0:t0 + 128, :], o_[:, :], R=[o_])
            else:
                b = l % 2
                S.dma("sp", self.xs[b][tt * 128:(tt + 1) * 128, :], o_[:, :], R=[o_], W=[self.r_xs[b][tt]])
        S.barrier()
        S.release(mk)

    def phase_na(self, l, need_ctx):
        S, T, NT, NTC, CTX, ROWS = self.S, self.T, self.NT, self.NTC, self.CTX, self.ROWS
        mk = S.mark()
        QT = S.sb("QT", [128, 8, T], BF16)
        KTt = S.sb("KTt", [128, 8, T], BF16)
        V = S.sb("V", [128, NT, 1024], BF16)
        yTn = S.sb("yTn", [128, 8, T], BF16)
        btab = S.sb("btab", [128, 8, 15, 64], BF16)
        gq = S.sb("gq", [128, 1])
        gk = S.sb("gk", [128, 1])
        S.dma("sp", gq[:, :], self.na_qn[l].rearrange("(p o) -> p o", o=1), W=[gq])
        S.dma("sp", gk[:, :], self.na_kn[l].rearrange("(p o) -> p o", o=1), W=[gk])
        S.ts("dve", gq[:, :], gq[:, :], float(NA_D) ** -0.5, None, ALU.mult, R=[gq], W=[gq])
        mk2 = S.mark()
        raw = [S.sb("raw", [128, 512]) for _ in range(2)]
        sq = [S.sb("sq", [128, 512]) for _ in range(2)]
        rs = [S.sb("rs", [128, 512]) for _ in range(2)]
        cnt = 0
        for qk in range(2):
            dst, gain = (QT, gq) if qk == 0 else (KTt, gk)
            for h in range(8):
                fi = qk * 8 + h
                for t0 in range(0, T, 512):
                    n = min(512, T - t0)
                    a, b, c = raw[cnt % 2], sq[cnt % 2], rs[cnt % 2]
                    pb = self.PB[cnt % 2]
                    cnt += 1
                    S.dma("sp", a[:, 0:n], self.pT[fi * 128:(fi + 1) * 128, t0:t0 + n], R=[self.r_pT[fi]], W=[a])
                    S.act(b[:, 0:n], a[:, 0:n], AF.Square, R=[a], W=[b])
                    S.mm(pb[:, 0:n], self.C("ones"), b[:, 0:n], R=[b, self.consts], W=[pb])
                    S.act(c[:, 0:n], pb[:, 0:n], AF.Sqrt, R=[pb], W=[c], bias=NORM_EPS, scale=1.0 / NA_D)
                    S.recip(c[:, 0:n], c[:, 0:n], R=[c], W=[c])
                    S.stt("dve", dst[:, h, t0:t0 + n], a[:, 0:n], gain[:, 0:1], c[:, 0:n], ALU.mult, ALU.mult, R=[a, gain, c], W=[dst])
        vst = [S.sb("vst", [128, 1024]) for _ in range(2)]
        for tt in range(NT):
            v_ = vst[tt % 2]
            S.dma("sp", v_[:, :], self.pk[tt * 128:(tt + 1) * 128, 0:1024], R=[self.r_pk[tt]], W=[v_])
            S.copy("pool", V[:, tt, :], v_[:, :], R=[v_], W=[V])
        btf = [S.sb("btf", [128, 15, 64]) for _ in range(2)]
        for h in range(8):
            b_ = btf[h % 2]
            S.memset("pool", b_[:, :, :], 0.0, W=[b_])
            S.dma("sp", b_[:, 0:7, :], self.rpb[l, h, 0:896, :].rearrange("(m p) w -> p m w", p=128), W=[b_])
            S.dma("sp", b_[0:64, 7, :], self.rpb[l, h, 896:960, :], W=[b_])
            S.dma("sp", b_[:, 8:15, :], self.rpb[l, h, 64:960, :].rearrange("(m p) w -> p m w", p=128), W=[b_])
            S.tt("dve", btab[:, h, :, :], b_[:, :, :], self.C("na_mask").unsqueeze(1).to_broadcast([128, 15, 64]), ALU.add,
                 R=[b_, self.consts], W=[btab])
        S.barrier()
        S.release(mk2)
        PT = [S.sb("PT", [128, 512], BF16) for _ in range(3)]
        zt = [S.sb("zt", [128, 8, 64]) for _ in range(2)]
        rinv = [S.sb("rinv", [128, 64]) for _ in range(3)]
        tmpo = [S.sb("tmpo", [128, 64]) for _ in range(3)]
        kh = min(8, ROWS)
        blocks = [("lat", r) for r in range(ROWS)]
        if need_ctx:
            blocks += [("ctx", cb) for cb in range(CTX // 64)]
        cnt = 0
        for bi, (kind, r) in enumerate(blocks):
            if kind == "lat":
                q0 = CTX + r * 64
                rs0 = min(max(r - kh // 2, 0), ROWS - kh)
                tiles = []
                for lt in range(rs0 // 2, (rs0 + kh - 1) // 2 + 1):
                    lo = 0 if 2 * lt >= rs0 else 64
                    hi = 128 if 2 * lt + 1 <= rs0 + kh - 1 else 64
                    ro0 = 2 * lt - r + 7
                    idx = ro0 // 2 if ro0 % 2 == 0 else 8 + (ro0 - 1) // 2
                    assert 0 <= idx < 15
                    tiles.append((NTC + lt, lo, hi, idx))
                for ct in range(NTC):
                    tiles.append((ct, 0, 128, None))
            else:
                q0 = r * 64
                tiles = [(ct, 0, 128, None) for ct in range(NTC)]
            nk = len(tiles)
            z_ = zt[bi % 2]
            S.dma("sp", z_[:, :, :], self.pT[16 * 128:24 * 128, q0:q0 + 64].rearrange("(h p) t -> p h t", p=128),
                  R=self.r_pT[16:24], W=[z_])
            for h in range(8):
                ps_s = self.PB[cnt % 3]
                ps_o = self.PB[3 + cnt % 3]
                pt_ = PT[cnt % 3]
                ri, to = rinv[cnt % 3], tmpo[cnt % 3]
                cnt += 1
                for i, (tt, lo, hi, idx) in enumerate(tiles):
                    S.mm(ps_s[:, i * 64:(i + 1) * 64], KTt[:, h, tt * 128:(tt + 1) * 128], QT[:, h, q0:q0 + 64], start=True, stop=(idx is None),
                         R=[KTt, QT], W=[ps_s])
                    if idx is not None:
                        S.mm(ps_s[:, i * 64:(i + 1) * 64], self.CB("ident"), btab[:, h, idx, :], start=False, stop=True,
                             R=[btab, self.consts_bf], W=[ps_s])
                S.act(pt_[:, 0:nk * 64], ps_s[:, 0:nk * 64], AF.Exp, R=[ps_s], W=[pt_])
                for i, (tt, lo, hi, idx) in enumerate(tiles):
                    if lo != 0:
                        S.memset("pool", pt_[0:lo, i * 64:(i + 1) * 64], 0.0, W=[pt_])
                    if hi != 128:
                        S.memset("pool", pt_[hi:128, i * 64:(i + 1) * 64], 0.0, W=[pt_])
                for i, (tt, lo, hi, idx) in enumerate(tiles):
                    S.mm(ps_o[:, 0:64], V[:, tt, h * 128:(h + 1) * 128], pt_[:, i * 64:(i + 1) * 64], start=(i == 0), stop=(i == nk - 1),
                         R=[V, pt_], W=[ps_o])
                for i, (tt, lo, hi, idx) in enumerate(tiles):
                    S.mm(ps_o[:, 64:128], self.CB("ones"), pt_[:, i * 64:(i + 1) * 64], start=(i == 0), stop=(i == nk - 1),
                         R=[self.consts_bf, pt_], W=[ps_o])
                S.recip(ri[:, :], ps_o[:, 64:128], R=[ps_o], W=[ri])
                S.tt("dve", to[:, :], ps_o[:, 0:64], ri[:, :], ALU.mult, R=[ps_o, ri], W=[to])
                S.tt("pool", yTn[:, h, q0:q0 + 64], to[:, :], z_[:, h, :], ALU.mult, R=[to, z_], W=[yTn])
        t_lo = 0 if need_ctx else CTX
        S.dma("sp", self.yT[0:1024, t_lo:T].rearrange("(h p) t -> p h t", p=128), yTn[:, :, t_lo:T], R=[yTn], W=self.r_yT[0:8])
        S.barrier()
        S.release(mk)

    def ring(self, name, n, shape, dtype=F32):
        bufs = [self.S.sb(name, shape, dtype) for _ in range(n)]
        state = {"i": 0}

        def get():
            b = bufs[state["i"] % n]
            state["i"] += 1
            return b
        return get

    def psum_slots(self, banks, width):
        slots = []
        per = 512 // width
        for q in range(per):
            for b in banks:
                slots.append((self.PB[b], q * width, self.PB[b].r))
        state = {"i": 0}

        def get():
            pb, c0, r = slots[state["i"] % len(slots)]
            state["i"] += 1
            return pb, c0, r
        return get

    def inv_doubling(self, M0, M0T, getbuf, getps, rconst):
        S = self.S
        ident = self.C("ident")
        P = getbuf()
        S.tt("pool", P[:, :], M0[:, :], ident, ALU.add, R=[M0, rconst], W=[P])
        M, MT = M0, M0T
        for k in range(6):
            newM = newMT = None
            if k <= 4:
                pb, c0, r = getps()
                S.mm(pb[:, c0:c0 + 128], MT[:, :], M[:, :], R=[MT, M], W=[r])
                newM = getbuf()
                S.copy("act", newM[:, :], pb[:, c0:c0 + 128], R=[r], W=[newM])
                pb, c0, r = getps()
                S.mm(pb[:, c0:c0 + 128], M[:, :], MT[:, :], R=[MT, M], W=[r])
                newMT = getbuf()
                S.copy("dve", newMT[:, :], pb[:, c0:c0 + 128], R=[r], W=[newMT])
            if k >= 1:
                pb, c0, r = getps()
                S.mm(pb[:, c0:c0 + 128], MT[:, :], P[:, :], start=True, stop=False, R=[MT, P], W=[r])
                S.mm(pb[:, c0:c0 + 128], ident, P[:, :], start=False, stop=True, R=[rconst, P], W=[r])
                newP = getbuf()
                S.copy("act" if k % 2 == 0 else "dve", newP[:, :], pb[:, c0:c0 + 128], R=[r], W=[newP])
                P = newP
            if newM is not None:
                M, MT = newM, newMT
            yield
        self._inv_result = P

    def phase_dn(self, l):
        S, T, NT, NTC, CTX, SEQ = self.S, self.T, self.NT, self.NTC, self.CTX, self.SEQ
        mk = S.mark()
        cst = self.consts
        ab = S.sb("ab", [128, NT, 16])
        S.dma("sp", ab[:, :, :], self.pk[:, 1024:1040].rearrange("(tt p) c -> p tt c", p=128), R=self.r_pk, W=[ab])
        expA = S.sb("expA", [128, 8])
        dtb = S.sb("dtb", [128, 8])
        S.dma("sp", expA[:, :], self.dn_A[l].partition_broadcast(128), W=[expA])
        S.dma("sp", dtb[:, :], self.dn_dt[l].partition_broadcast(128), W=[dtb])
        S.act(expA[:, :], expA[:, :], AF.Exp, R=[expA], W=[expA])
        G = S.sb("G", [128, NT, 8])
        Bt = S.sb("Bt", [128, NT, 8])
        NB = S.sb("NB", [128, NT, 8])
        GC = S.sb("GC", [128, NT, 8])
        EG = S.sb("EG", [128, NT, 8])
        BG = S.sb("BG", [128, NT, 8])
        ED = S.sb("ED", [128, NT, 8])
        GL = S.sb("GL", [128, 2, NT, 8])
        S.tt("dve", G[:, :, :], ab[:, :, 0:8], dtb[:, :].unsqueeze(1).to_broadcast([128, NT, 8]), ALU.add, R=[ab, dtb], W=[G])
        S.act(G[:, :, :], G[:, :, :], AF.Exp, R=[G], W=[G])
        S.act(G[:, :, :], G[:, :, :], AF.Ln, R=[G], W=[G], bias=1.0)
        S.stt("dve", G[:, :, :], G[:, :, :], -1.0, expA[:, :].unsqueeze(1).to_broadcast([128, NT, 8]), ALU.mult, ALU.mult, R=[G, expA], W=[G])
        S.act(Bt[:, :, :], ab[:, :, 8:16], AF.Sigmoid, R=[ab], W=[Bt])
        S.ts("dve", NB[:, :, :], Bt[:, :, :], -1.0, None, ALU.mult, R=[Bt], W=[NB])
        pb = self.PB[0]
        for d, tri in ((0, "tri_f"), (1, "tri_b")):
            S.mm(pb[:, d * NT * 4:(d + 1) * NT * 4], self.C(tri), G[:, :, d * 4:(d + 1) * 4], R=[G, cst], W=[pb])
        S.copy("dve", GC[:, :, 0:4], pb[:, 0:NT * 4].rearrange("p (t c) -> p t c", c=4), R=[pb], W=[GC])
        S.copy("dve", GC[:, :, 4:8], pb[:, NT * 4:NT * 8].rearrange("p (t c) -> p t c", c=4), R=[pb], W=[GC])
        pb = self.PB[1]
        S.mm(pb[:, 0:NT * 8], self.C("blk"), G[:, :, :], R=[G, cst], W=[pb])
        S.tt("dve", ED[:, :, :], pb[:, 0:NT * 8].rearrange("p (t c) -> p t c", c=8), GC[:, :, :], ALU.subtract, R=[pb, GC], W=[ED])
        S.act(ED[:, :, :], ED[:, :, :], AF.Exp, R=[ED], W=[ED])
        S.act(EG[:, :, :], GC[:, :, :], AF.Exp, R=[GC], W=[EG])
        S.tt("dve", BG[:, :, :], EG[:, :, :], Bt[:, :, :], ALU.mult, R=[EG, Bt], W=[BG])
        pb = self.PB[2]
        for j in range(2):
            S.mm(pb[:, j * NT * 8:(j + 1) * NT * 8], self.C("ind%d" % j), G[:, :, :], R=[G, cst], W=[pb])
        S.act(GL[:, :, :, :], pb[:, 0:2 * NT * 8].rearrange("p (j t c) -> p j t c", j=2, c=8), AF.Exp, R=[pb], W=[GL])
        import os
        CUT = int(os.environ.get("DN_CUT", "99"))
        if CUT == 0:
            S.barrier(); S.release(mk); return
        cw = S.sb("cw", [128, 12, 5])
        for j in range(5):
            S.dma("sp", cw[:, :, j], self.dn_conv[l, j].rearrange("(t p) -> p t", p=128), W=[cw], slow=True)
        ng = S.sb("ng", [128, 1])
        S.dma("sp", ng[:, :], self.dn_norm[l].rearrange("(p o) -> p o", o=1), W=[ng])
        getps = [self.psum_slots([0, 1, 2, 3], 128), self.psum_slots([4, 5, 6, 7], 128)]
        W_ = CTX + 4 + SEQ
        qkv = [S.sb("qkvT", [128, T]) for _ in range(3)]
        cnt = 0
        for h in range(DN_H):
            mk1 = S.mark()
            rope = S.sb("rope", [128, 2, SEQ])
            S.dma("sp", rope[:, :, :], self.rope_in, W=[rope])
            rawp = [S.sb("rawp", [128, W_ + 4]) for _ in range(2)]
            for r_ in rawp:
                S.memset("pool", r_[:, :], 0.0, W=[r_])
            acc = [S.sb("acc", [128, W_]) for _ in range(2)]
            xs_ = S.sb("xs_", [128, T])
            sq = S.sb("sq", [128, T])
            rin = [S.sb("rin", [128, 512]) for _ in range(2)]
            t1 = [S.sb("t1", [128, 512]) for _ in range(2)]
            t2 = [S.sb("t2", [128, 512]) for _ in range(2)]
            for s3 in range(3):
                ct = s3 * 4 + h
                fi = 24 + ct
                rp, ac = rawp[cnt % 2], acc[cnt % 2]
                eng = "dve"
                cnt += 1
                S.dma("sp", rp[:, 2:2 + CTX], self.pT[fi * 128:(fi + 1) * 128, 0:CTX], R=[self.r_pT[fi]], W=[rp])
                S.dma("sp", rp[:, CTX + 6:CTX + 6 + SEQ], self.pT[fi * 128:(fi + 1) * 128, CTX:T], R=[self.r_pT[fi]], W=[rp])
                S.ts(eng, ac[:, :], rp[:, 0:W_], cw[:, ct, 0:1], None, ALU.mult, R=[rp, cw], W=[ac])
                for j in range(1, 5):
                    S.stt(eng, ac[:, :], rp[:, j:j + W_], cw[:, ct, j:j + 1], ac[:, :], ALU.mult, ALU.add, R=[rp, cw, ac], W=[ac])
                dst = qkv[s3]
                if s3 == 2:
                    S.act(dst[:, 0:CTX], ac[:, 0:CTX], AF.Silu, R=[ac], W=[dst])
                    S.act(dst[:, CTX:T], ac[:, CTX + 4:W_], AF.Silu, R=[ac], W=[dst])
                    continue
                S.act(xs_[:, 0:CTX], ac[:, 0:CTX], AF.Silu, R=[ac], W=[xs_])
                S.act(xs_[:, CTX:T], ac[:, CTX + 4:W_], AF.Silu, R=[ac], W=[xs_])
                S.act(sq[:, :], xs_[:, :], AF.Square, R=[xs_], W=[sq])
                scale = float(DN_D) ** -0.5 if s3 == 0 else 1.0
                for bi, t0 in enumerate(range(0, T, 512)):
                    n = min(512, T - t0)
                    pb = self.PB[bi % 2]
                    ri = rin[bi % 2]
                    S.mm(pb[:, 0:n], self.C("ones"), sq[:, t0:t0 + n], R=[sq, cst], W=[pb])
                    S.act(ri[:, 0:n], pb[:, 0:n], AF.Sqrt, R=[pb], W=[ri], bias=NORM_EPS, scale=1.0)
                    S.recip(ri[:, 0:n], ri[:, 0:n], R=[ri], W=[ri])
                    S.stt("dve", xs_[:, t0:t0 + n], xs_[:, t0:t0 + n], scale, ri[:, 0:n], ALU.mult, ALU.mult, R=[xs_, ri], W=[xs_])
                S.copy("pool", dst[:, 0:CTX], xs_[:, 0:CTX], R=[xs_], W=[dst])
                for bi, t0 in enumerate(range(0, SEQ, 512)):
                    n = min(512, SEQ - t0)
                    pb = self.PB[2 + bi % 2]
                    a_, b_ = t1[bi % 2], t2[bi % 2]
                    S.mm(pb[:, 0:n], self.C("Rt"), xs_[:, CTX + t0:CTX + t0 + n], R=[xs_, cst], W=[pb])
                    S.tt("pool", a_[:, 0:n], xs_[:, CTX + t0:CTX + t0 + n], rope[:, 0, t0:t0 + n], ALU.mult, R=[xs_, rope], W=[a_])
                    S.tt("dve", b_[:, 0:n], pb[:, 0:n], rope[:, 1, t0:t0 + n], ALU.mult, R=[pb, rope], W=[b_])
                    S.tt("pool", dst[:, CTX + t0:CTX + t0 + n], a_[:, 0:n], b_[:, 0:n], ALU.add, R=[a_, b_], W=[dst])
            qT, kT, vT = qkv
            S.barrier()
            S.release(mk1)
            if CUT == 1:
                S.release(mk); return
            mk2 = S.mark()
            oT = [S.sb("oT", [128, T]) for _ in range(2)]
            Sst = [S.sb("Sst", [128, 128]) for _ in range(2)]
            getbuf = [self.ring("dnb%d" % d, 44, [128, 128]) for d in range(2)]
            zt = S.sb("zt", [128, T])
            sq = S.sb("sq", [128, T])
            rin = [S.sb("rin", [128, 512]) for _ in range(2)]
            for d in range(2):
                S.memset("pool", Sst[d][:, :], 0.0, W=[Sst[d]])

            def stream(d):
                gb, gp = getbuf[d], getps[d]
                col = d * 4 + h
                st = Sst[d]
                if d == 0:
                    order = list(range(NT))
                    PMn, NMn = "PM_LI", "NM_UI"
                else:
                    order = list(range(NTC - 1, -1, -1)) + list(range(NT - 1, NTC - 1, -1))
                    PMn, NMn = "PM_UI", "NM_LI"
                ident = self.C("ident")
                for tt in order:
                    tok = slice(tt * 128, (tt + 1) * 128)
                    gc = GC[:, tt, col:col + 1]
                    dg = gb()
                    S.ts("dve", dg[:, :], ident, gc, None, ALU.mult, R=[cst, GC], W=[dg])
                    pg, cg, rg = gp()
                    S.mm(pg[:, cg:cg + 128], self.C("ones"), dg[:, :], R=[dg, cst], W=[rg])
                    tD, Dm, tDT, DT, EGr = gb(), gb(), gb(), gb(), gb()
                    S.stt("dve", tD[:, :], pg[:, cg:cg + 128], gc, self.C(PMn), ALU.subtract, ALU.add, R=[rg, GC, cst], W=[tD])
                    S.act(Dm[:, :], tD[:, :], AF.Exp, R=[tD], W=[Dm], scale=-1.0)
                    S.stt("dve", tDT[:, :], pg[:, cg:cg + 128], gc, self.C(NMn), ALU.subtract, ALU.add, R=[rg, GC, cst], W=[tDT])
                    S.act(DT[:, :], tDT[:, :], AF.Exp, R=[tDT], W=[DT])
                    S.act(EGr[:, :], pg[:, cg:cg + 128], AF.Exp, R=[rg], W=[EGr])
                    yield
                    if CUT == 2:
                        continue
                    pk_, ck, rk = gp()
                    S.mm(pk_[:, ck:ck + 128], kT[:, tok], kT[:, tok], R=[kT], W=[rk])
                    pq, cq, rq = gp()
                    S.mm(pq[:, cq:cq + 128], kT[:, tok], qT[:, tok], R=[kT, qT], W=[rq])
                    nL, M0T, attnT = gb(), gb(), gb()
                    S.stt("dve", nL[:, :], pk_[:, ck:ck + 128], NB[:, tt, col:col + 1], Dm[:, :], ALU.mult, ALU.mult, R=[rk, NB, Dm], W=[nL])
                    S.tt("pool", M0T[:, :], nL[:, :], self.C("OFFD"), ALU.mult, R=[nL, cst], W=[M0T])
                    S.tt("dve", attnT[:, :], pq[:, cq:cq + 128], DT[:, :], ALU.mult, R=[rq, DT], W=[attnT])
                    pt_, c_t, rt = gp()
                    S.tr(pt_[:, c_t:c_t + 128], M0T[:, :], ident, R=[M0T, cst], W=[rt])
                    M0 = gb()
                    S.copy("act", M0[:, :], pt_[:, c_t:c_t + 128], R=[rt], W=[M0])
                    yield
                    if CUT == 3:
                        continue
                    yield from self.inv_doubling(M0, M0T, gb, gp, cst)
                    TT_ = self._inv_result
                    if CUT == 4:
                        continue
                    pkt, ckt, rkt = gp()
                    S.tr(pkt[:, ckt:ckt + 128], kT[:, tok], ident, R=[kT, cst], W=[rkt])
                    kbg, kdec, vb = gb(), gb(), gb()
                    S.ts("dve", kbg[:, :], pkt[:, ckt:ckt + 128], BG[:, tt, col:col + 1], None, ALU.mult, R=[rkt, BG], W=[kbg])
                    S.ts("dve", kdec[:, :], pkt[:, ckt:ckt + 128], ED[:, tt, col:col + 1], None, ALU.mult, R=[rkt, ED], W=[kdec])
                    pvt, cvt, rvt = gp()
                    S.tr(pvt[:, cvt:cvt + 128], vT[:, tok], ident, R=[vT, cst], W=[rvt])
                    S.ts("dve", vb[:, :], pvt[:, cvt:cvt + 128], Bt[:, tt, col:col + 1], None, ALU.mult, R=[rvt, Bt], W=[vb])
                    yield
                    pu, cu, ru = gp()
                    S.mm(pu[:, cu:cu + 128], TT_[:, :], vb[:, :], R=[TT_, vb], W=[ru])
                    U0, nWT, qdT, vnew = gb(), gb(), gb(), gb()
                    S.copy("act", U0[:, :], pu[:, cu:cu + 128], R=[ru], W=[U0])
                    pw, cw_, rw = gp()
                    S.mm(pw[:, cw_:cw_ + 128], kbg[:, :], TT_[:, :], R=[TT_, kbg], W=[rw])
                    S.ts("dve", nWT[:, :], pw[:, cw_:cw_ + 128], -1.0, None, ALU.mult, R=[rw], W=[nWT])
                    S.tt("pool", qdT[:, :], qT[:, tok], EGr[:, :], ALU.mult, R=[qT, EGr], W=[qdT])
                    yield
                    for j in ((0, 1) if d == 0 else (1, 0)):
                        cj = slice(64 * j, 64 * j + 64)
                        pv, cv, rv = gp()
                        S.mm(pv[:, cv:cv + 128], nWT[:, :], st[:, :], R=[nWT, st], W=[rv])
                        S.tt("dve", vnew[cj, :], U0[cj, :], pv[cj, cv:cv + 128], ALU.add, R=[U0, rv], W=[vnew])
                        po, co, ro = gp()
                        S.mm(po[:, co:co + 64], st[:, :], qdT[:, cj], start=True, stop=False, R=[st, qdT], W=[ro])
                        S.mm(po[:, co:co + 64], vnew[cj, :], attnT[cj, cj], start=False, stop=True, R=[vnew, attnT], W=[ro])
                        S.copy("act", oT[d][:, tt * 128 + 64 * j:tt * 128 + 64 * j + 64], po[:, co:co + 64], R=[ro], W=[oT[d]])
                        ps2, c2, r2 = gp()
                        S.mm(ps2[:, c2:c2 + 128], kdec[cj, :], vnew[cj, :], R=[kdec, vnew], W=[r2])
                        S.stt("dve", st[:, :], st[:, :], GL[:, j, tt, col:col + 1], ps2[:, c2:c2 + 128], ALU.mult, ALU.add, R=[st, GL, r2], W=[st])
                        yield

            interleave([stream(0), stream(1)])
            fi = 36 + h
            S.dma("sp", zt[:, :], self.pT[fi * 128:(fi + 1) * 128, :], R=[self.r_pT[fi]], W=[zt])
            S.tt("pool", oT[0][:, :], oT[0][:, :], oT[1][:, :], ALU.add, R=[oT[0], oT[1]], W=[oT[0]])
            S.act(sq[:, :], oT[0][:, :], AF.Square, R=[oT[0]], W=[sq])
            yb = S.sb("yb", [128, T], BF16)
            for bi, t0 in enumerate(range(0, T, 512)):
                n = min(512, T - t0)
                pb = self.PB[bi % 2]
                ri = rin[bi % 2]
                S.mm(pb[:, 0:n], self.C("ones"), sq[:, t0:t0 + n], R=[sq, cst], W=[pb])
                S.act(ri[:, 0:n], pb[:, 0:n], AF.Sqrt, R=[pb], W=[ri], bias=NORM_EPS, scale=1.0 / DN_D)
                S.recip(ri[:, 0:n], ri[:, 0:n], R=[ri], W=[ri])
                S.stt("dve", ri[:, 0:n], oT[0][:, t0:t0 + n], ng[:, 0:1], ri[:, 0:n], ALU.mult, ALU.mult, R=[oT[0], ng, ri], W=[ri])
                S.tt("dve", yb[:, t0:t0 + n], ri[:, 0:n], zt[:, t0:t0 + n], ALU.mult, R=[ri, zt], W=[yb])
            S.dma("sp", self.yT[(8 + h) * 128:(9 + h) * 128, :], yb[:, :], R=[yb], W=[self.r_yT[8 + h]])
            S.barrier()
            S.release(mk2)
        S.release(mk)

    def inv_doubling_b(self, M0, M0T, getbuf, getbank, rconst):
        S = self.S
        ident = self.C("ident")
        P = getbuf()
        S.tt("dve", P[:, :, :], M0[:, :, :], ident.unsqueeze(1).to_broadcast([128, 4, 128]), ALU.add, R=[M0, rconst], W=[P])
        M, MT = M0, M0T
        for k in range(6):
            newM = newMT = None
            if k <= 4:
                pb = getbank()
                for hh in range(4):
                    S.mm(pb[:, hh * 128:(hh + 1) * 128], MT[:, hh, :], M[:, hh, :], R=[MT, M], W=[pb])
                newM = getbuf()
                S.copy("act", newM[:, :, :], pb[:, :].rearrange("p (h c) -> p h c", c=128), R=[pb], W=[newM])
                pb = getbank()
                for hh in range(4):
                    S.mm(pb[:, hh * 128:(hh + 1) * 128], M[:, hh, :], MT[:, hh, :], R=[MT, M], W=[pb])
                newMT = getbuf()
                S.copy("dve", newMT[:, :, :], pb[:, :].rearrange("p (h c) -> p h c", c=128), R=[pb], W=[newMT])
            if k >= 1:
                pb = getbank()
                for hh in range(4):
                    S.mm(pb[:, hh * 128:(hh + 1) * 128], MT[:, hh, :], P[:, hh, :], start=True, stop=False, R=[MT, P], W=[pb])
                    S.mm(pb[:, hh * 128:(hh + 1) * 128], ident, P[:, hh, :], start=False, stop=True, R=[rconst, P], W=[pb])
                newP = getbuf()
                S.copy("act" if k % 2 == 0 else "dve", newP[:, :, :], pb[:, :].rearrange("p (h c) -> p h c", c=128), R=[pb], W=[newP])
                P = newP
            if newM is not None:
                M, MT = newM, newMT
            yield
        self._inv_result_b = P

    def phase_rw(self, l):
        S, T, NT, NTC, CTX, SEQ = self.S, self.T, self.NT, self.NTC, self.CTX, self.SEQ
        mk = S.mark()
        cst = self.consts
        C1 = -float(np.exp(-0.5))
        ro_d = [self.nc.dram_tensor("ro%d_%d" % (l, d), [T, 512], F32).ap() for d in range(2)]
        r_ro = [[Res("ro%d_%d" % (d, t)) for t in range(NT)] for d in range(2)]
        import os
        RWCUT = int(os.environ.get("RW_CUT", "99"))
        BCUT = int(os.environ.get("RW_BCUT", "99"))
        HG = 4
        CW = HG * 64
        XW = 4 * CW + 192
        ident = self.C("ident")
        for hg in range(2):
            mkg = S.mark()
            c0 = hg * CW
            mu = S.sb("mu", [128, XW])
            S.dma("sp", mu[:, 0:4 * CW].rearrange("p (s c) -> p s c", c=CW),
                  self.rw_mu[l, 0:2048].rearrange("(s c) -> s c", c=512)[:, c0:c0 + CW].partition_broadcast(128), W=[mu])
            S.dma("sp", mu[:, 4 * CW:XW], self.rw_mu[l, 2048:2240].partition_broadcast(128), W=[mu])

            def bc(name, src):
                t_ = S.sb(name, [128, CW])
                S.dma("sp", t_[:, :], src.partition_broadcast(128), W=[t_])
                return t_
            w0 = [bc("w0", self.rw_w0[l, d, c0:c0 + CW]) for d in range(2)]
            a0 = [bc("a0", self.rw_a0[l, d, c0:c0 + CW]) for d in range(2)]
            k_k = bc("k_k", self.rw_k_k[l, c0:c0 + CW])
            k_a = bc("k_a", self.rw_k_a[l, c0:c0 + CW])
            r_k = bc("r_k", self.rw_r_k[l, c0:c0 + CW])
            ln_w = bc("ln_w", self.rw_ln_w[l, c0:c0 + CW])
            ln_b = bc("ln_b", self.rw_ln_b[l, c0:c0 + CW])
            wup = S.sb("wup", [96, 2, CW])
            aup = S.sb("aup", [96, 2, CW])
            for d in range(2):
                S.dma("sp", wup[:, d, :], self.rw_w_up[l, d, :, c0:c0 + CW], W=[wup])
                S.dma("sp", aup[:, d, :], self.rw_a_up[l, d, :, c0:c0 + CW], W=[aup])
            Sst = [S.sb("Sst", [64, HG, 64]) for _ in range(2)]
            for d in range(2):
                S.memset("pool", Sst[d][:, :, :], 0.0, W=[Sst[d]])

            def stream(d):
                st = Sst[d]
                banks = [0, 1, 2, 3] if d == 0 else [4, 5, 6, 7]
                bstate = {"i": 0}

                def gbank():
                    b = self.PB[banks[bstate["i"] % 4]]
                    bstate["i"] += 1
                    return b
                cur = S.sb("cur", [128, XW])
                prv = S.sb("prv", [128, XW])
                nxt = S.sb("nxt", [128, XW])
                g2 = self.ring("g2_%d" % d, 22, [128, CW])
                g3 = self.ring("g3_%d" % d, 9, [128, HG, 128])
                g3k = self.ring("g3k_%d" % d, 3, [128, HG, 128])
                gT = self.ring("gT_%d" % d, 4, [64, HG, 128])
                sm = self.ring("sm_%d" % d, 10, [128, HG])
                if d == 0:
                    order = list(range(NT))
                    tri, LSn, USn, LIn, UIn = "tri_f", "LS", "US", "LI", "UI"
                else:
                    order = list(range(NTC - 1, -1, -1)) + list(range(NT - 1, NTC - 1, -1))
                    tri, LSn, USn, LIn, UIn = "tri_b", "US", "LS", "UI", "LI"

                def mb(name):
                    return self.C(name).unsqueeze(1).to_broadcast([128, HG, 128])

                def v3(ap):
                    return ap.rearrange("p (h n) -> p h n", n=64)

                for tt in order:
                    g0 = tt * 128
                    first = tt == 0 or tt == NTC
                    lastt = tt == NTC - 1 or tt == NT - 1

                    def ld(dst, r0, r1, p0):
                        n = r1 - r0
                        S.dma("sp", dst[p0:p0 + n, 0:4 * CW].rearrange("p (s c) -> p s c", c=CW),
                              self.pk[r0:r1, 1040:1040 + 2048].rearrange("t (s c) -> t s c", c=512)[:, :, c0:c0 + CW],
                              R=self.r_pk[max(tt - 1, 0):min(tt + 2, NT)], W=[dst])
                        S.dma("sp", dst[p0:p0 + n, 4 * CW:XW], self.pk[r0:r1, 1040 + 2048:1040 + 2240],
                              R=self.r_pk[max(tt - 1, 0):min(tt + 2, NT)], W=[dst])
                    ld(cur, g0, g0 + 128, 0)
                    if first:
                        S.memset("pool", prv[:, :], 0.0, W=[prv])
                        ld(prv, g0, g0 + 127, 1)
                    else:
                        ld(prv, g0 - 1, g0 + 127, 0)
                    if lastt:
                        S.memset("pool", nxt[:, :], 0.0, W=[nxt])
                        ld(nxt, g0 + 1, g0 + 128, 0)
                    else:
                        ld(nxt, g0 + 1, g0 + 129, 0)
                    S.tt("pool", prv[:, :], prv[:, :], nxt[:, :], ALU.add, R=[prv, nxt], W=[prv])
                    S.stt("dve", prv[:, :], prv[:, :], 0.5, cur[:, :], ALU.mult, ALU.subtract, R=[prv, cur], W=[prv])
                    S.tt("pool", prv[:, :], prv[:, :], mu[:, :], ALU.mult, R=[prv, mu], W=[prv])
                    S.tt("dve", cur[:, :], cur[:, :], prv[:, :], ALU.add, R=[cur, prv], W=[cur])
                    xr, xk, xv, xz = (cur[:, i * CW:(i + 1) * CW] for i in range(4))
                    xwd, xad = cur[:, 4 * CW:4 * CW + 96], cur[:, 4 * CW + 96:XW]
                    yield
                    if RWCUT == 0:
                        continue
                    th = g2()
                    S.act(th[:, 0:96], xwd, AF.Tanh, R=[cur], W=[th])
                    pb = gbank()
                    S.tr(pb[0:96, 0:128], th[:, 0:96], ident, R=[th, cst], W=[pb])
                    S.tr(pb[0:96, 128:256], xad, ident, R=[cur, cst], W=[pb])
                    lT = g2()
                    S.copy("act", lT[0:96, 0:256], pb[0:96, 0:256], R=[pb], W=[lT])
                    pbw = gbank()
                    S.mm(pbw[:, 0:CW], lT[0:96, 0:128], wup[:, d, :], R=[lT, wup], W=[pbw])
                    S.mm(pbw[:, CW:2 * CW], lT[0:96, 128:256], aup[:, d, :], R=[lT, aup], W=[pbw])
                    sg, a_ = g2(), g2()
                    S.tt("dve", sg[:, :], pbw[:, 0:CW], w0[d][:, :], ALU.add, R=[pbw, w0[d]], W=[sg])
                    S.act(sg[:, :], sg[:, :], AF.Sigmoid, R=[sg], W=[sg])
                    S.tt("dve", a_[:, :], pbw[:, CW:2 * CW], a0[d][:, :], ALU.add, R=[pbw, a0[d]], W=[a_])
                    S.act(a_[:, :], a_[:, :], AF.Sigmoid, R=[a_], W=[a_])
                    yield
                    if RWCUT == 1:
                        continue
                    kk, sq_ = g2(), g2()
                    ss, ss2 = sm(), sm()
                    S.tt("pool", kk[:, :], xk, k_k[:, :], ALU.mult, R=[cur, k_k], W=[kk])
                    S.act(sq_[:, :], kk[:, :], AF.Square, R=[kk], W=[sq_])
                    S.op("dve", lambda e, o=ss, i=sq_: e.tensor_reduce(out=o[:, :], in_=v3(i[:, :]), axis=AX.X, op=ALU.add), R=[sq_], W=[ss])
                    S.act(ss[:, :], ss[:, :], AF.Sqrt, R=[ss], W=[ss], bias=NORM_EPS, scale=1.0)
                    S.recip(ss[:, :], ss[:, :], R=[ss], W=[ss])
                    S.tt("dve", v3(kk[:, :]), v3(kk[:, :]), ss[:, :].unsqueeze(2).to_broadcast([128, HG, 64]), ALU.mult, R=[kk, ss], W=[kk])
                    kd, b_, rk = g2(), g2(), g2()
                    S.stt("dve", kd[:, :], a_[:, :], -1.0, k_a[:, :], ALU.add, ALU.mult, R=[a_, k_a], W=[kd])
                    S.stt("dve", kd[:, :], kd[:, :], 1.0, xk, ALU.add, ALU.mult, R=[kd, cur], W=[kd])
                    S.tt("pool", b_[:, :], kk[:, :], a_[:, :], ALU.mult, R=[kk, a_], W=[b_])
                    S.tt("pool", rk[:, :], xr, r_k[:, :], ALU.mult, R=[cur, r_k], W=[rk])
                    S.tt("pool", rk[:, :], rk[:, :], kd[:, :], ALU.mult, R=[rk, kd], W=[rk])
                    S.op("dve", lambda e, o=ss2, i=rk: e.tensor_reduce(out=o[:, :], in_=v3(i[:, :]), axis=AX.X, op=ALU.add), R=[rk], W=[ss2])
                    yield
                    if RWCUT == 2:
                        continue
                    pbc = gbank()
                    S.mm(pbc[:, 0:CW], self.C(tri), sg[:, :], R=[sg, cst], W=[pbc])
                    S.mm(pbc[:, CW:2 * CW], self.C("blk"), sg[:, :], R=[sg, cst], W=[pbc])
                    cum, tmp, rt_, kt_, bt_, at_, Kd, nBd = g2(), g2(), g2(), g2(), g2(), g2(), g2(), g2()
                    S.copy("act", cum[:, :], pbc[:, 0:CW], R=[pbc], W=[cum])
                    S.act(tmp[:, :], pbc[:, 0:CW], AF.Exp, R=[pbc], W=[tmp], scale=C1)
                    S.tt("dve", rt_[:, :], xr, tmp[:, :], ALU.mult, R=[cur, tmp], W=[rt_])
                    tmp = g2()
                    S.act(tmp[:, :], pbc[:, 0:CW], AF.Exp, R=[pbc], W=[tmp], scale=-C1)
                    S.tt("dve", kt_[:, :], kd[:, :], tmp[:, :], ALU.mult, R=[kd, tmp], W=[kt_])
                    S.tt("pool", bt_[:, :], b_[:, :], tmp[:, :], ALU.mult, R=[b_, tmp], W=[bt_])
                    tmp = g2()
                    S.tt("dve", tmp[:, :], cum[:, :], sg[:, :], ALU.subtract, R=[cum, sg], W=[tmp])
                    S.act(tmp[:, :], tmp[:, :], AF.Exp, R=[tmp], W=[tmp], scale=C1)
                    S.tt("pool", at_[:, :], kk[:, :], tmp[:, :], ALU.mult, R=[kk, tmp], W=[at_])
                    tmp = g2()
                    S.tt("dve", tmp[:, :], pbc[:, CW:2 * CW], cum[:, :], ALU.subtract, R=[pbc, cum], W=[tmp])
                    S.act(tmp[:, :], tmp[:, :], AF.Exp, R=[tmp], W=[tmp], scale=C1)
                    S.tt("dve", Kd[:, :], kd[:, :], tmp[:, :], ALU.mult, R=[kd, tmp], W=[Kd])
                    S.stt("dve", nBd[:, :], b_[:, :], -1.0, tmp[:, :], ALU.mult, ALU.mult, R=[b_, tmp], W=[nBd])
                    pbp = gbank()
                    for hh in range(HG):
                        S.mm(pbp[0:64, hh * 2:hh * 2 + 2], sg[:, hh * 64:(hh + 1) * 64], self.C("ind2"), R=[sg, cst], W=[pbp])
                    PCc = g2()
                    S.act(PCc[0:64, 0:2 * HG], pbp[0:64, 0:2 * HG], AF.Exp, R=[pbp], W=[PCc], scale=C1)
                    yield
                    if RWCUT == 3:
                        continue
                    XT = []
                    for src in (at_, bt_, kt_, rt_):
                        pbt = gbank()
                        for hh in range(HG):
                            S.tr(pbt[0:64, hh * 128:(hh + 1) * 128], src[:, hh * 64:(hh + 1) * 64], ident, R=[src, cst], W=[pbt])
                        x_ = gT()
                        S.copy("act" if len(XT) % 2 == 0 else "dve", x_[:, :, :], pbt[0:64, :].rearrange("p (h c) -> p h c", c=128), R=[pbt], W=[x_])
                        XT.append(x_)
                    aT, bT, kT, rT = XT
                    yield
                    if RWCUT == 4:
                        continue
                    def scores(lh, rh, mask, neg, keep=False):
                        pbs = gbank()
                        for hh in range(HG):
                            S.mm(pbs[:, hh * 128:(hh + 1) * 128], lh[:, hh, :], rh[:, hh, :], R=[lh, rh], W=[pbs])
                        o_ = g3k() if keep else g3()
                        pv_ = pbs[:, :].rearrange("p (h c) -> p h c", c=128)
                        if neg:
                            S.stt("dve", o_[:, :, :], pv_, -1.0, mb(mask), ALU.mult, ALU.mult, R=[pbs, cst], W=[o_])
                        else:
                            S.tt("dve", o_[:, :, :], pv_, mb(mask), ALU.mult, R=[pbs, cst], W=[o_])
                        return o_
                    M0T = scores(aT, bT, LSn, True)
                    M0 = scores(bT, aT, USn, True)
                    AakT = scores(kT, aT, USn, False, True)
                    yield
                    if RWCUT == 5:
                        continue
                    ArkT = scores(kT, rT, UIn, False, True)
                    nArbT = scores(bT, rT, UIn, True, True)
                    yield
                    if RWCUT == 6:
                        continue
                    yield from self.inv_doubling_b(M0, M0T, g3, gbank, cst)
                    TT_ = self._inv_result_b
                    pbx = gbank()
                    for hh in range(HG):
                        S.mm(pbx[:, hh * 64:(hh + 1) * 64], AakT[:, hh, :], cur[:, 2 * CW + hh * 64:2 * CW + (hh + 1) * 64], R=[AakT, cur], W=[pbx])
                    X_ = g2()
                    S.copy("act", X_[:, :], pbx[:, 0:CW], R=[pbx], W=[X_])
                    pbu = gbank()
                    for hh in range(HG):
                        S.mm(pbu[:, hh * 64:(hh + 1) * 64], TT_[:, hh, :], X_[:, hh * 64:(hh + 1) * 64], R=[TT_, X_], W=[pbu])
                    U0, U_ = g2(), g2()
                    S.copy("dve", U0[:, :], pbu[:, 0:CW], R=[pbu], W=[U0])
                    pbw2 = gbank()
                    for hh in range(HG):
                        S.mm(pbw2[0:64, hh * 128:(hh + 1) * 128], at_[:, hh * 64:(hh + 1) * 64], TT_[:, hh, :], R=[at_, TT_], W=[pbw2])
                    WT = gT()
                    S.copy("act", WT[:, :, :], pbw2[0:64, :].rearrange("p (h c) -> p h c", c=128), R=[pbw2], W=[WT])
                    otok = g2()
                    yield
                    if RWCUT == 7:
                        continue
                    for j in ((0, 1) if d == 0 else (1, 0)):
                        cj = slice(64 * j, 64 * j + 64)
                        pb1 = gbank()
                        for hh in range(HG):
                            S.mm(pb1[:, hh * 64:(hh + 1) * 64], WT[:, hh, :], st[:, hh, :], R=[WT, st], W=[pb1])
                        S.tt("dve", U_[cj, :], U0[cj, :], pb1[cj, 0:CW], ALU.add, R=[U0, pb1], W=[U_])
                        if BCUT == 1:
                            continue
                        pb2a = gbank()
                        for hh in range(HG):
                            hs = slice(hh * 64, (hh + 1) * 64)
                            S.mm(pb2a[:, hs], rT[:, hh, :], st[:, hh, :], R=[rT, st], W=[pb2a])
                        pb2 = gbank()
                        for hh in range(HG):
                            hs = slice(hh * 64, (hh + 1) * 64)
                            S.mm(pb2[:, hs], ArkT[cj, hh, :], cur[cj, 2 * CW + hh * 64:2 * CW + (hh + 1) * 64], R=[ArkT, cur], W=[pb2])
                        pb2c = gbank()
                        for hh in range(HG):
                            hs = slice(hh * 64, (hh + 1) * 64)
                            S.mm(pb2c[:, hs], nArbT[cj, hh, :], U_[cj, hs], R=[nArbT, U_], W=[pb2c])
                        S.copy("act", otok[cj, :], pb2a[cj, 0:CW], R=[pb2a], W=[otok])
                        S.tt("dve", otok[cj, :], otok[cj, :], pb2[cj, 0:CW], ALU.add, R=[otok, pb2], W=[otok])
                        S.tt("dve", otok[cj, :], otok[cj, :], pb2c[cj, 0:CW], ALU.add, R=[otok, pb2c], W=[otok])
                        if BCUT == 3:
                            continue
                        pb3 = gbank()
                        for hh in range(HG):
                            hs = slice(hh * 64, (hh + 1) * 64)
                            S.mm(pb3[0:64, hs], Kd[cj, hs], cur[cj, 2 * CW + hh * 64:2 * CW + (hh + 1) * 64], R=[Kd, cur], W=[pb3])
                            S.mm(pb3[0:64, CW + hh * 64:CW + (hh + 1) * 64], nBd[cj, hs], U_[cj, hs], R=[nBd, U_], W=[pb3])
                        if BCUT == 4:
                            continue
                        S.tt("dve", st[:, :, :], st[:, :, :], PCc[0:64, 0:2 * HG].rearrange("p (h j) -> p h j", j=2)[:, :, j:j + 1].to_broadcast([64, HG, 64]),
                             ALU.mult, R=[st, PCc], W=[st])
                        S.tt("dve", st[:, :, :], st[:, :, :], pb3[0:64, 0:CW].rearrange("p (h v) -> p h v", v=64), ALU.add, R=[st, pb3], W=[st])
                        S.tt("dve", st[:, :, :], st[:, :, :], pb3[0:64, CW:2 * CW].rearrange("p (h v) -> p h v", v=64), ALU.add, R=[st, pb3], W=[st])
                        yield
                    if RWCUT == 8:
                        continue
                    mean, var = sm(), sm()
                    cen, sq2, bon = g2(), g2(), g2()
                    S.op("dve", lambda e, o=mean, i=otok: e.tensor_reduce(out=o[:, :], in_=v3(i[:, :]), axis=AX.X, op=ALU.add), R=[otok], W=[mean])
                    S.ts("dve", mean[:, :], mean[:, :], 1.0 / 64, None, ALU.mult, R=[mean], W=[mean])
                    S.tt("dve", v3(cen[:, :]), v3(otok[:, :]), mean[:, :].unsqueeze(2).to_broadcast([128, HG, 64]), ALU.subtract, R=[otok, mean], W=[cen])
                    S.act(sq2[:, :], cen[:, :], AF.Square, R=[cen], W=[sq2])
                    S.op("dve", lambda e, o=var, i=sq2: e.tensor_reduce(out=o[:, :], in_=v3(i[:, :]), axis=AX.X, op=ALU.add), R=[sq2], W=[var])
                    S.act(var[:, :], var[:, :], AF.Sqrt, R=[var], W=[var], bias=RW_GN_EPS, scale=1.0 / 64)
                    S.recip(var[:, :], var[:, :], R=[var], W=[var])
                    S.tt("dve", v3(cen[:, :]), v3(cen[:, :]), var[:, :].unsqueeze(2).to_broadcast([128, HG, 64]), ALU.mult, R=[cen, var], W=[cen])
                    S.tt("pool", cen[:, :], cen[:, :], ln_w[:, :], ALU.mult, R=[cen, ln_w], W=[cen])
                    S.tt("pool", cen[:, :], cen[:, :], ln_b[:, :], ALU.add, R=[cen, ln_b], W=[cen])
                    S.tt("dve", v3(bon[:, :]), v3(xv), ss2[:, :].unsqueeze(2).to_broadcast([128, HG, 64]), ALU.mult, R=[cur, ss2], W=[bon])
                    S.tt("pool", cen[:, :], cen[:, :], bon[:, :], ALU.add, R=[cen, bon], W=[cen])
                    S.act(sq2[:, :], xz, AF.Silu, R=[cur], W=[sq2])
                    S.tt("dve", cen[:, :], cen[:, :], sq2[:, :], ALU.mult, R=[cen, sq2], W=[cen])
                    S.dma("sp", ro_d[d][g0:g0 + 128, c0:c0 + CW], cen[:, :], R=[cen], W=[r_ro[d][tt]])
                    yield
                    if RWCUT == 9:
                        continue

            interleave([stream(0), stream(1)])
            S.barrier()
            S.release(mkg)
        ra = [S.sb("ra", [128, 512]) for _ in range(2)]
        rb = [S.sb("rb", [128, 512]) for _ in range(2)]
        yTr = S.sb("yTr", [128, 4, T], BF16)
        for tt in range(NT):
            a_, b_ = ra[tt % 2], rb[tt % 2]
            S.dma("sp", a_[:, :], ro_d[0][tt * 128:(tt + 1) * 128, :], R=[r_ro[0][tt]], W=[a_])
            S.dma("sp", b_[:, :], ro_d[1][tt * 128:(tt + 1) * 128, :], R=[r_ro[1][tt]], W=[b_])
            S.tt("pool", a_[:, :], a_[:, :], b_[:, :], ALU.add, R=[a_, b_], W=[a_])
            pb = self.PB[tt % 4]
            for q in range(4):
                S.tr(pb[:, q * 128:(q + 1) * 128], a_[:, q * 128:(q + 1) * 128], ident, R=[a_, cst], W=[pb])
            S.copy("act" if tt % 2 == 0 else "dve", yTr[:, :, tt * 128:(tt + 1) * 128], pb[:, :].rearrange("p (q t) -> p q t", t=128), R=[pb], W=[yTr])
        S.dma("sp", self.yT[1536:2048, :].rearrange("(q p) t -> p q t", p=128), yTr[:, :, :], R=[yTr], W=self.r_yT[12:16])
        S.barrier()
        S.release(mk)


def make_in_maps(inputs, bld, n_cores):
    f = lambda a: np.ascontiguousarray(np.asarray(a, dtype=np.float32))
    L = bld.DEPTH
    wk = np.arange(64)[:, None]
    wq = np.arange(64)[None, :]
    idx = np.clip(wk - wq + 15, 0, 30)
    rpb = np.asarray(inputs["na_rpb"], np.float32)
    rpb_exp = f(rpb[:, :, :, idx].reshape(L, NA_H, 15 * 64, 64))
    shared = {
        "w_mod": f(inputs["w_mod"]), "b_mod": f(inputs["b_mod"]), "norm_g": f(inputs["norm_g"]),
        "w_in": f(inputs["w_in"]), "w_out": f(inputs["w_out"]),
        "na_qn": f(inputs["na_qn"]), "na_kn": f(inputs["na_kn"]), "rpb_exp": rpb_exp,
        "dn_conv": f(inputs["dn_conv"]), "dn_A_log": f(np.asarray(inputs["dn_A_log"]).reshape(L, 8)),
        "dn_dt_bias": f(np.asarray(inputs["dn_dt_bias"]).reshape(L, 8)), "dn_norm": f(inputs["dn_norm"]),
        "rw_mu": f(inputs["rw_mu"]), "rw_w0": f(inputs["rw_w0"]), "rw_w_up": f(inputs["rw_w_up"]),
        "rw_a0": f(inputs["rw_a0"]), "rw_a_up": f(inputs["rw_a_up"]), "rw_k_k": f(inputs["rw_k_k"]),
        "rw_k_a": f(inputs["rw_k_a"]), "rw_r_k": f(np.asarray(inputs["rw_r_k"]).reshape(L, 512)),
        "rw_ln_w": f(inputs["rw_ln_w"]), "rw_ln_b": f(inputs["rw_ln_b"]),
        "consts": f(bld.consts_np), "rope": f(bld.rope_np),
    }
    x = np.asarray(inputs["x"], np.float32)
    ctx = np.asarray(inputs["ctx"], np.float32)
    c = np.asarray(inputs["c"], np.float32)
    c_ctx = np.asarray(inputs["c_ctx"], np.float32)
    maps = []
    for b in range(n_cores):
        m = dict(shared)
        m["x"] = f(x[b])
        m["ctx"] = f(ctx[b])
        m["cc"] = f(np.stack([c[b], c_ctx]))
        maps.append(m)
    return maps


def kernel(**inputs):
    bld = Builder()
    nc = bld.build()
    maps = make_in_maps(inputs, bld, 8)
    res = run_bass_kernel_spmd(nc, maps, core_ids=list(range(8)))
    return np.stack([np.asarray(r["out"], np.float32) for r in res.results], axis=0)
```

```python
import numpy as np
import ml_dtypes
import concourse.bass as bass
import concourse.mybir as mybir
from concourse.bass_utils import run_bass_kernel_spmd

F32 = mybir.dt.float32
BF16 = mybir.dt.bfloat16
ALU = mybir.AluOpType
AF = mybir.ActivationFunctionType
AX = mybir.AxisListType

NA_H, NA_D = 8, 128
DN_H, DN_D = 4, 128
RW_H, RW_N = 8, 64
GRID_W = 64
PROJ = 8400
MIX = 2048
NORM_EPS = 1e-6
RW_GN_EPS = 64e-5
BIG = 1e30
DN_BF16 = False


class Res:
    __slots__ = ("name", "w", "rs")

    def __init__(self, name=""):
        self.name = name
        self.w = None
        self.rs = {}


class TT:
    def __init__(self, t, name=""):
        self.t = t
        self.r = Res(name)

    def __getitem__(self, k):
        return self.t[k]


def _res(xs):
    out = []
    for x in xs:
        if x is None:
            continue
        out.append(x.r if isinstance(x, TT) else x)
    return out


class Sched:
    ENGS = ("pe", "act", "dve", "pool", "sp")
    DMAQ = ("sp", "pool", "act")

    def __init__(self, nc, n_dma_sems=14):
        self.nc = nc
        self.ops = {e: [] for e in self.ENGS}
        self.cnt = {e: 0 for e in self.ENGS}
        self.known = {e: {} for e in self.ENGS}
        self.n_dma_sems = n_dma_sems
        self.dma_cnt = {}
        self.dma_rr = {q: 0 for q in self.DMAQ}
        self.sems = {}
        self.stack = []
        self.uid = 0

    def enter(self, cm):
        v = cm.__enter__()
        self.stack.append(cm)
        return v

    def sb(self, name, shape, dtype=F32):
        self.uid += 1
        return TT(self.enter(self.nc.sbuf_tensor("%s_%d" % (name, self.uid), list(shape), dtype)), name)

    def ps(self, name, shape, dtype=F32):
        self.uid += 1
        return TT(self.enter(self.nc.psum_tensor("%s_%d" % (name, self.uid), list(shape), dtype)), name)

    def mark(self):
        return len(self.stack)

    def release(self, mark):
        while len(self.stack) > mark:
            self.stack.pop().__exit__(None, None, None)

    def _record(self, eng, fn, reads, writes, token, extra=()):
        deps = {}

        def need(k, v):
            if deps.get(k, 0) < v:
                deps[k] = v

        for r in reads:
            if r.w is not None:
                need(*r.w)
        for w in writes:
            if w.w is not None:
                need(*w.w)
            for k, v in w.rs.items():
                need(k, v)
        for k, v in extra:
            need(k, v)
        waits = []
        kn = self.known[eng]
        for k, v in deps.items():
            if k == eng and eng == "pe":
                continue
            if kn.get(k, 0) >= v:
                continue
            kn[k] = v
            waits.append((k, v))
        self.ops[eng].append((waits, fn, token))
        k, v = token
        for r in reads:
            if r.rs.get(k, 0) < v:
                r.rs[k] = v
        for w in writes:
            w.w = token
            w.rs = {}

    def op(self, eng, fn, R=(), W=()):
        self.cnt[eng] += 1
        token = (eng, self.cnt[eng])
        self._record(eng, fn, _res(R), _res(W), token)

    def dma(self, q, out, in_, R=(), W=(), slow=False):
        i = self.dma_rr[q]
        self.dma_rr[q] = (i + 1) % self.n_dma_sems
        key = "d_%s_%d" % (q, i)
        prev = self.dma_cnt.get(key, 0)
        self.dma_cnt[key] = prev + 16
        token = (key, prev + 16)
        extra = [(key, prev)] if prev > 0 else []
        if slow:
            fn = lambda e: e.dma_start(out=out, in_=in_, allow_slow_non_contiguous=True)
        else:
            fn = lambda e: e.dma_start(out=out, in_=in_)
        self._record(q, fn, _res(R), _res(W), token, extra)

    def barrier(self):
        cur = {}
        for e in self.ENGS:
            if self.cnt[e] > 0:
                cur[e] = self.cnt[e]
        for k, v in self.dma_cnt.items():
            cur[k] = v
        for e in self.ENGS:
            waits = []
            kn = self.known[e]
            for k, v in cur.items():
                if kn.get(k, 0) >= v:
                    continue
                kn[k] = v
                waits.append((k, v))
            if waits:
                self.ops[e].append((waits, None, None))

    def mm(self, out, lhsT, rhs, start=True, stop=True, R=(), W=()):
        self.op("pe", lambda e: e.matmul(out, lhsT=lhsT, rhs=rhs, start=start, stop=stop), R, W)

    def tr(self, out, in_, ident, R=(), W=()):
        self.op("pe", lambda e: e.transpose(out=out, in_=in_, identity=ident), R, W)

    def act(self, out, in_, func, R=(), W=(), **kw):
        self.op("act", lambda e: e.activation(out=out, in_=in_, func=func, **kw), R, W)

    def tt(self, eng, out, in0, in1, op, R=(), W=()):
        self.op(eng, lambda e: e.tensor_tensor(out=out, in0=in0, in1=in1, op=op), R, W)

    def ts(self, eng, out, in0, s1, s2, op0, op1=None, R=(), W=()):
        if op1 is None:
            self.op(eng, lambda e: e.tensor_scalar(out=out, in0=in0, scalar1=s1, scalar2=None, op0=op0), R, W)
        else:
            self.op(eng, lambda e: e.tensor_scalar(out=out, in0=in0, scalar1=s1, scalar2=s2, op0=op0, op1=op1), R, W)

    def stt(self, eng, out, in0, scalar, in1, op0, op1, R=(), W=()):
        self.op(eng, lambda e: e.scalar_tensor_tensor(out=out, in0=in0, scalar=scalar, in1=in1, op0=op0, op1=op1), R, W)

    def copy(self, eng, out, in_, R=(), W=()):
        if eng == "act":
            self.op("act", lambda e: e.activation(out=out, in_=in_, func=AF.Copy), R, W)
        else:
            self.op(eng, lambda e: e.tensor_copy(out=out, in_=in_), R, W)

    def memset(self, eng, out, val, W=()):
        self.op(eng, lambda e: e.memset(out, val), (), W)

    def recip(self, out, in_, R=(), W=()):
        self.op("dve", lambda e: e.reciprocal(out=out, in_=in_), R, W)

    def emit(self):
        nc = self.nc
        keys = list(self.ENGS) + sorted(self.dma_cnt.keys())
        for k in keys:
            self.sems[k] = self.enter(nc.semaphore("s_" + k))
        block = self.enter(nc.Block())
        sems = self.sems

        def run(engname):
            def body(eng):
                for waits, fn, token in self.ops[engname]:
                    for k, v in waits:
                        eng.wait_ge(sems[k], v)
                    if fn is None:
                        continue
                    ins = fn(eng)
                    k, v = token
                    ins.then_inc(sems[k], 16 if k.startswith("d_") else 1)
            return body

        block.tensor(run("pe"))
        block.scalar(run("act"))
        block.vector(run("dve"))
        block.gpsimd(run("pool"))
        block.sync(run("sp"))
        while self.stack:
            self.stack.pop().__exit__(None, None, None)


def interleave(gens, lead=None):
    gens = list(gens)
    if lead:
        for g, n in zip(list(gens), lead):
            for _ in range(n):
                try:
                    next(g)
                except StopIteration:
                    gens.remove(g)
                    break
    while gens:
        for g in list(gens):
            try:
                next(g)
            except StopIteration:
                gens.remove(g)


def build_consts():
    cols = {}
    parts = []

    def add(name, arr):
        arr = np.asarray(arr, np.float32)
        assert arr.shape[0] == 128
        cols[name] = (sum(p.shape[1] for p in parts), arr.shape[1])
        parts.append(arr)

    i = np.arange(128)[:, None]
    j = np.arange(128)[None, :]
    same = (i // 64) == (j // 64)
    add("ident", (i == j))
    add("ones", np.ones((128, 128)))
    add("tri_f", same & (i <= j))
    add("tri_b", same & (i >= j))
    add("blk", same)
    add("ind0", np.broadcast_to(i < 64, (128, 128)))
    add("ind1", np.broadcast_to(i >= 64, (128, 128)))
    add("ind2", np.concatenate([(i < 64), (i >= 64)], axis=1))
    add("LS", same & (i > j))
    add("US", same & (i < j))
    add("LI", same & (i >= j))
    add("UI", same & (i <= j))
    add("PM_LI", np.where(same & (i >= j), 0.0, BIG))
    add("PM_UI", np.where(same & (i <= j), 0.0, BIG))
    add("NM_LI", np.where(same & (i >= j), 0.0, -BIG))
    add("NM_UI", np.where(same & (i <= j), 0.0, -BIG))
    add("OFFD", (i != j))
    Rt = np.zeros((128, 128), np.float32)
    for base in (0, 64):
        for m in range(32):
            Rt[base + m + 32, base + m] = -1.0
            Rt[base + m, base + m + 32] = 1.0
    add("Rt", Rt)
    wq = np.arange(64)
    cs = np.clip(wq - 8, 0, 48)
    wk = np.arange(64)
    ok = (wk[:, None] >= cs[None, :]) & (wk[:, None] < cs[None, :] + 16)
    mneg = np.where(ok, 0.0, -BIG).astype(np.float32)
    add("na_mask", np.concatenate([mneg, mneg], axis=0))
    return np.concatenate(parts, axis=1), cols


def build_rope(SEQ):
    nf = 32
    inv = 10000.0 ** (-np.arange(nf, dtype=np.float32) / nf)
    t = np.arange(SEQ)
    pos_r = (t // GRID_W).astype(np.float32)
    pos_c = (t % GRID_W).astype(np.float32)
    ang = np.zeros((128, SEQ), np.float32)
    for d in range(128):
        pos = pos_r if d < 64 else pos_c
        ang[d] = pos * inv[d % 32]
    return np.stack([np.cos(ang), np.sin(ang)], axis=1).astype(np.float32)


class Builder:
    def __init__(self, D=2048, SEQ=2048, CTX=256, DEPTH=2, debug=False, phases=None):
        self.D, self.SEQ, self.CTX, self.DEPTH = D, SEQ, CTX, DEPTH
        self.T = SEQ + CTX
        self.NT = self.T // 128
        self.NTC = CTX // 128
        self.KT = D // 128
        self.ROWS = SEQ // GRID_W
        self.debug = debug
        self.phases = phases
        self.consts_np, self.ccols = build_consts()
        self.rope_np = build_rope(SEQ)

    def on(self, ph):
        return self.phases is None or ph in self.phases

    def declare(self):
        nc = self.nc
        D, SEQ, CTX, L = self.D, self.SEQ, self.CTX, self.DEPTH

        def inp(name, shape):
            return nc.dram_tensor(name, list(shape), F32, kind="ExternalInput").ap()

        self.x_in = inp("x", [SEQ, D])
        self.ctx_in = inp("ctx", [CTX, D])
        self.cc_in = inp("cc", [2, D])
        self.w_mod = inp("w_mod", [L, D, 3 * D])
        self.b_mod = inp("b_mod", [L, 3 * D])
        self.norm_g = inp("norm_g", [L, D])
        self.w_in = inp("w_in", [L, D, PROJ])
        self.w_out = inp("w_out", [L, MIX, D])
        self.na_qn = inp("na_qn", [L, 128])
        self.na_kn = inp("na_kn", [L, 128])
        self.rpb = inp("rpb_exp", [L, NA_H, 15 * 64, 64])
        self.dn_conv = inp("dn_conv", [L, 5, 1536])
        self.dn_A = inp("dn_A_log", [L, 8])
        self.dn_dt = inp("dn_dt_bias", [L, 8])
        self.dn_norm = inp("dn_norm", [L, 128])
        self.rw_mu = inp("rw_mu", [L, 2240])
        self.rw_w0 = inp("rw_w0", [L, 2, 512])
        self.rw_w_up = inp("rw_w_up", [L, 2, 96, 512])
        self.rw_a0 = inp("rw_a0", [L, 2, 512])
        self.rw_a_up = inp("rw_a_up", [L, 2, 96, 512])
        self.rw_k_k = inp("rw_k_k", [L, 512])
        self.rw_k_a = inp("rw_k_a", [L, 512])
        self.rw_r_k = inp("rw_r_k", [L, 512])
        self.rw_ln_w = inp("rw_ln_w", [L, 512])
        self.rw_ln_b = inp("rw_ln_b", [L, 512])
        self.consts_in = inp("consts", list(self.consts_np.shape))
        self.rope_in = inp("rope", [128, 2, SEQ])
        self.out = nc.dram_tensor("out", [SEQ, D], F32, kind="ExternalOutput").ap()

        dk = "ExternalOutput" if self.debug else "Internal"
        T = self.T
        self.mod_d = nc.dram_tensor("mod_d", [L, 2, 3 * D], F32, kind=dk).ap()
        self.pT = nc.dram_tensor("pT", [40 * 128, T], F32, kind=dk).ap()
        self.pk = nc.dram_tensor("pk", [T, 3280], F32, kind=dk).ap()
        self.yT = nc.dram_tensor("yT", [MIX, T], BF16, kind=dk).ap()
        self.xs = [nc.dram_tensor("xs%d" % i, [T, D], F32, kind=dk).ap() for i in range(2)]
        self.r_mod = Res("mod_d")
        self.r_pT = [Res("pT%d" % i) for i in range(40)]
        self.r_pk = [Res("pk%d" % i) for i in range(self.NT)]
        self.r_yT = [Res("yT%d" % i) for i in range(16)]
        self.r_xs = [[Res("xs%d_%d" % (i, t)) for t in range(self.NT)] for i in range(2)]

    def C(self, name, rows=slice(0, 128), c0=0, n=None):
        o, w = self.ccols[name]
        if n is None:
            n = w - c0
        return self.consts[rows, o + c0:o + c0 + n]

    def CB(self, name, rows=slice(0, 128), c0=0, n=None):
        o, w = self.ccols[name]
        if n is None:
            n = w - c0
        return self.consts_bf[rows, o + c0:o + c0 + n]

    def build(self):
        self.nc = nc = bass.Bass("TRN2", target_bir_lowering=False)
        self.declare()
        self.S = S = Sched(nc)
        NCON = self.consts_np.shape[1]
        self.consts = S.sb("consts", [128, NCON], F32)
        self.consts_bf = S.sb("consts_bf", [128, NCON], BF16)
        S.dma("sp", self.consts[:, :], self.consts_in, W=[self.consts])
        S.copy("dve", self.consts_bf[:, :], self.consts[:, :], R=[self.consts], W=[self.consts_bf])
        self.PB = [S.ps("pb%d" % i, [128, 512], F32) for i in range(8)]
        S.barrier()
        self.marks = []

        def mark(name):
            self.marks.append((name, dict(S.cnt)))
        for l in range(self.DEPTH):
            if self.on("mod"):
                mark("mod%d" % l)
                self.phase_mod(l)
        for l in range(self.DEPTH):
            last = l == self.DEPTH - 1
            if self.on("proj"):
                mark("proj%d" % l)
                self.phase_norm_proj(l)
            if self.on("na"):
                mark("na%d" % l)
                self.phase_na(l, need_ctx=not last)
            if self.on("dn"):
                mark("dn%d" % l)
                self.phase_dn(l)
            if self.on("rw"):
                mark("rw%d" % l)
                self.phase_rw(l)
            if self.on("out"):
                mark("out%d" % l)
                self.phase_out(l, last)
        mark("end")
        S.barrier()
        S.emit()
        return nc

    def src_rows(self, l, tt):
        if l == 0:
            if tt < self.NTC:
                return self.ctx_in[tt * 128:(tt + 1) * 128, :], None
            t0 = (tt - self.NTC) * 128
            return self.x_in[t0:t0 + 128, :], None
        b = (l - 1) % 2
        return self.xs[b][tt * 128:(tt + 1) * 128, :], self.r_xs[b][tt]

    def phase_mod(self, l):
        S, D, KT = self.S, self.D, self.KT
        mk = S.mark()
        cT = S.sb("cT", [128, 2, KT])
        for m in range(2):
            S.dma("sp", cT[:, m, :], self.cc_in[m].rearrange("(kt p) -> p kt", p=128), W=[cT], slow=True)
        S.act(cT[:, :, :], cT[:, :, :], AF.Silu, R=[cT], W=[cT])
        bm = S.sb("bm", [2, 3 * D])
        S.dma("sp", bm[:, :], self.b_mod[l].partition_broadcast(2), W=[bm])
        msb = S.sb("msb", [2, 3 * D])
        wm = [S.sb("wm", [128, KT, 512]) for _ in range(2)]
        wv = self.w_mod[l].rearrange("(kt p) n -> p kt n", p=128)
        for g, g0 in enumerate(range(0, 3 * D, 512)):
            n = min(512, 3 * D - g0)
            w = wm[g % 2]
            S.dma("sp", w[:, :, 0:n], wv[:, :, g0:g0 + n], W=[w])
            ps = self.PB[g % 2]
            for kt in range(KT):
                S.mm(ps[0:2, 0:n], cT[:, :, kt], w[:, kt, 0:n], start=(kt == 0), stop=(kt == KT - 1), R=[cT, w], W=[ps])
            S.tt("dve", msb[:, g0:g0 + n], ps[0:2, 0:n], bm[:, g0:g0 + n], ALU.add, R=[ps, bm], W=[msb])
        S.dma("sp", self.mod_d[l], msb[:, :], R=[msb], W=[self.r_mod])
        S.barrier()
        S.release(mk)

    def fm_cols(self):
        return list(range(0, 2048, 128)) + list(range(3072, 4096, 128)) + list(range(4096, 6144, 128))

    def tm_groups(self):
        g = []
        for c0 in range(2048, 3072, 256):
            g.append((c0, 256, c0 - 2048))
        g.append((6144, 16, 1024))
        for c0 in range(6160, 8400, 256):
            g.append((c0, min(256, 8400 - c0), 1040 + c0 - 6160))
        return g

    def phase_norm_proj(self, l):
        S, D, KT, NT, T = self.S, self.D, self.KT, self.NT, self.T
        mk = S.mark()
        hT = S.sb("hT", [128, KT, T], BF16)
        r_hT = [Res("hT%d" % t) for t in range(NT)]
        mk2 = S.mark()
        AB = {}
        gbc = S.sb("gbc", [128, D])
        S.dma("sp", gbc[:, :], self.norm_g[l].partition_broadcast(128), W=[gbc])
        for which, row in (("lat", 0), ("ctx", 1)):
            A = S.sb("A" + which, [128, D])
            B = S.sb("B" + which, [128, D])
            S.dma("sp", A[:, :], self.mod_d[l, row, D:2 * D].partition_broadcast(128), R=[self.r_mod], W=[A])
            S.dma("sp", B[:, :], self.mod_d[l, row, 0:D].partition_broadcast(128), R=[self.r_mod], W=[B])
            S.stt("dve", A[:, :], A[:, :], 1.0, gbc[:, :], ALU.add, ALU.mult, R=[A, gbc], W=[A])
            AB[which] = (A, B)
        xt = [S.sb("xt", [128, D]) for _ in range(2)]
        tmp = S.sb("tmp", [128, D])
        hb = [S.sb("hb", [128, D], BF16) for _ in range(2)]
        ss = [S.sb("ss", [128, 1]) for _ in range(2)]
        for tt in range(NT):
            src, rsrc = self.src_rows(l, tt)
            x_ = xt[tt % 2]
            h_ = hb[tt % 2]
            s_ = ss[tt % 2]
            A, B = AB["ctx" if tt < self.NTC else "lat"]
            S.dma("sp", x_[:, :], src, R=[rsrc], W=[x_])
            S.memset("pool", s_[:, :], 0.0, W=[s_])
            S.act(tmp[:, :], x_[:, :], AF.Square, R=[x_, s_], W=[tmp, s_], accum_out=s_[:, :])
            S.act(s_[:, :], s_[:, :], AF.Sqrt, R=[s_], W=[s_], bias=NORM_EPS, scale=1.0 / D)
            S.recip(s_[:, :], s_[:, :], R=[s_], W=[s_])
            S.stt("dve", tmp[:, :], x_[:, :], s_[:, 0:1], A[:, :], ALU.mult, ALU.mult, R=[x_, s_, A], W=[tmp])
            S.tt("dve", h_[:, :], tmp[:, :], B[:, :], ALU.add, R=[tmp, B], W=[h_])
            for k0 in range(0, KT, 8):
                nk = min(8, KT - k0)
                pb = self.PB[(tt * 2 + k0 // 8) % 4]
                pv = pb[:, :].bitcast(BF16)
                for k in range(nk):
                    S.tr(pv[:, k * 128:(k + 1) * 128], h_[:, (k0 + k) * 128:(k0 + k + 1) * 128], self.CB("ident"), R=[h_, self.consts_bf], W=[pb])
                S.copy("act" if (k0 // 8) % 2 == 0 else "dve", hT[:, k0:k0 + nk, tt * 128:(tt + 1) * 128],
                       pv[:, 0:nk * 128].rearrange("p (k t) -> p k t", t=128), R=[pb], W=[r_hT[tt]])
        S.barrier()
        S.release(mk2)
        wv = self.w_in[l].rearrange("(kt p) n -> p kt n", p=128)
        wf = [S.sb("wf", [128, KT, 256]) for _ in range(2)]
        wb = [S.sb("wb", [128, KT, 256], BF16) for _ in range(2)]
        stage = [S.sb("stage", [128, T]) for _ in range(2)]
        fm = self.fm_cols()
        silu_tiles = set(range(16, 24)) | set(range(36, 40))
        nblk = (T + 511) // 512
        stg = [S.sb("stg", [128, 256]) for _ in range(3)]
        groups = [("fm", fm[g0], 256, g0) for g0 in range(0, len(fm), 2)] + [("tm", c0, n, pc0) for (c0, n, pc0) in self.tm_groups()]
        for g0 in range(0, len(fm), 2):
            assert fm[g0 + 1] == fm[g0] + 128

        def load(gi):
            kind, c0, n, _ = groups[gi]
            wf_, wb_ = wf[gi % 2], wb[gi % 2]
            S.dma("sp", wf_[:, :, 0:n], wv[:, :, c0:c0 + n], W=[wf_])
            S.copy("pool", wb_[:, :, 0:n], wf_[:, :, 0:n], R=[wf_], W=[wb_])
        cnt = 0
        load(0)
        for gi, (kind, c0, n, extra) in enumerate(groups):
            if gi + 1 < len(groups):
                load(gi + 1)
            wb_ = wb[gi % 2]
            if kind == "fm":
                g0 = extra
                for ct in range(2):
                    fi = g0 + ct
                    st = stage[fi % 2]
                    for b in range(nblk):
                        t0 = b * 512
                        nn = min(512, T - t0)
                        pb = self.PB[cnt % 4 + 4]
                        cnt += 1
                        rr = [r_hT[t] for t in range(t0 // 128, (t0 + nn) // 128)]
                        for kt in range(KT):
                            S.mm(pb[:, 0:nn], wb_[:, kt, ct * 128:(ct + 1) * 128], hT[:, kt, t0:t0 + nn], start=(kt == 0), stop=(kt == KT - 1),
                                 R=[wb_] + rr, W=[pb])
                        if fi in silu_tiles:
                            S.act(st[:, t0:t0 + nn], pb[:, 0:nn], AF.Silu, R=[pb], W=[st])
                        elif cnt % 2 == 0:
                            S.copy("act", st[:, t0:t0 + nn], pb[:, 0:nn], R=[pb], W=[st])
                        else:
                            S.copy("dve", st[:, t0:t0 + nn], pb[:, 0:nn], R=[pb], W=[st])
                    S.dma("sp", self.pT[fi * 128:(fi + 1) * 128, :], st[:, :], R=[st], W=[self.r_pT[fi]])
            else:
                pc0 = extra
                for tt in range(NT):
                    pb = self.PB[cnt % 4 + 4]
                    sg = stg[cnt % 3]
                    cnt += 1
                    for kt in range(KT):
                        S.mm(pb[:, 0:n], hT[:, kt, tt * 128:(tt + 1) * 128], wb_[:, kt, 0:n], start=(kt == 0), stop=(kt == KT - 1),
                             R=[wb_, r_hT[tt]], W=[pb])
                    S.copy("act" if cnt % 2 == 0 else "dve", sg[:, 0:n], pb[:, 0:n], R=[pb], W=[sg])
                    S.dma("sp", self.pk[tt * 128:(tt + 1) * 128, pc0:pc0 + n], sg[:, 0:n], R=[sg], W=[self.r_pk[tt]])
        S.barrier()
        S.release(mk)

    def phase_out(self, l, last):
        S, D, NT, T = self.S, self.D, self.NT, self.T
        mk = S.mark()
        MT = MIX // 128
        wo = S.sb("wo", [128, MT, D], BF16)
        wst = [S.sb("wst", [128, D]) for _ in range(2)]
        wv = self.w_out[l].rearrange("(mt p) n -> p mt n", p=128)
        for i in range(MT):
            w_ = wst[i % 2]
            S.dma("sp", w_[:, :], wv[:, i, :], W=[w_])
            S.copy("pool", wo[:, i, :], w_[:, :], R=[w_], W=[wo])
        TB = 512
        ysb = [S.sb("ysb", [128, MT, TB], BF16) for _ in range(2)]
        yv = self.yT.rearrange("(mt p) t -> p mt t", p=128)
        G = {}
        for which, row in (("lat", 0), ("ctx", 1)):
            g_ = S.sb("G" + which, [128, D])
            S.dma("sp", g_[:, :], self.mod_d[l, row, 2 * D:3 * D].partition_broadcast(128), R=[self.r_mod], W=[g_])
            G[which] = g_
        xt = [S.sb("xt", [128, D]) for _ in range(2)]
        xo = [S.sb("xo", [128, D]) for _ in range(2)]
        cnt = 0
        cur_blk = -1
        for tt in range(NT):
            if last and tt < self.NTC:
                continue
            blk = (tt * 128) // TB
            if blk != cur_blk:
                cur_blk = blk
                yb_ = ysb[blk % 2]
                nb = min(TB, T - blk * TB)
                S.dma("sp", yb_[:, :, 0:nb], yv[:, :, blk * TB:blk * TB + nb], R=self.r_yT, W=[yb_])
            yo = tt * 128 - blk * TB
            src, rsrc = self.src_rows(l, tt)
            x_, o_ = xt[tt % 2], xo[tt % 2]
            g_ = G["ctx" if tt < self.NTC else "lat"]
            S.dma("sp", x_[:, :], src, R=[rsrc], W=[x_])
            for c0 in range(0, D, 512):
                n = min(512, D - c0)
                pb = self.PB[cnt % 4]
                cnt += 1
                for mt in range(MT):
                    S.mm(pb[:, 0:n], yb_[:, mt, yo:yo + 128], wo[:, mt, c0:c0 + n], start=(mt == 0), stop=(mt == MT - 1),
                         R=[yb_, wo], W=[pb])
                S.tt("dve", o_[:, c0:c0 + n], pb[:, 0:n], g_[:, c0:c0 + n], ALU.mult, R=[pb, g_], W=[o_])
            S.tt("pool", o_[:, :], o_[:, :], x_[:, :], ALU.add, R=[o_, x_], W=[o_])
            if last:
                t0 = (tt - self.NTC) * 128
                S.dma("sp", self.out[t0:t0 + 128, :], o_[:, :], R=[o_])
            else:
                b = l % 2
                S.dma("sp", self.xs[b][tt * 128:(tt + 1) * 128, :], o_[:, :], R=[o_], W=[self.r_xs[b][tt]])
        S.barrier()
        S.release(mk)

    def phase_na(self, l, need_ctx):
        S, T, NT, NTC, CTX, ROWS = self.S, self.T, self.NT, self.NTC, self.CTX, self.ROWS
        mk = S.mark()
        QT = S.sb("QT", [128, 8, T], BF16)
        KTt = S.sb("KTt", [128, 8, T], BF16)
        V = S.sb("V", [128, NT, 1024], BF16)
        yTn = S.sb("yTn", [128, 8, T], BF16)
        btab = S.sb("btab", [128, 8, 15, 64], BF16)
        gq = S.sb("gq", [128, 1])
        gk = S.sb("gk", [128, 1])
        S.dma("sp", gq[:, :], self.na_qn[l].rearrange("(p o) -> p o", o=1), W=[gq])
        S.dma("sp", gk[:, :], self.na_kn[l].rearrange("(p o) -> p o", o=1), W=[gk])
        S.ts("dve", gq[:, :], gq[:, :], float(NA_D) ** -0.5, None, ALU.mult, R=[gq], W=[gq])
        mk2 = S.mark()
        raw = [S.sb("raw", [128, 512]) for _ in range(2)]
        sq = [S.sb("sq", [128, 512]) for _ in range(2)]
        rs = [S.sb("rs", [128, 512]) for _ in range(2)]
        cnt = 0
        for qk in range(2):
            dst, gain = (QT, gq) if qk == 0 else (KTt, gk)
            for h in range(8):
                fi = qk * 8 + h
                for t0 in range(0, T, 512):
                    n = min(512, T - t0)
                    a, b, c = raw[cnt % 2], sq[cnt % 2], rs[cnt % 2]
                    pb = self.PB[cnt % 2]
                    cnt += 1
                    S.dma("sp", a[:, 0:n], self.pT[fi * 128:(fi + 1) * 128, t0:t0 + n], R=[self.r_pT[fi]], W=[a])
                    S.act(b[:, 0:n], a[:, 0:n], AF.Square, R=[a], W=[b])
                    S.mm(pb[:, 0:n], self.C("ones"), b[:, 0:n], R=[b, self.consts], W=[pb])
                    S.act(c[:, 0:n], pb[:, 0:n], AF.Sqrt, R=[pb], W=[c], bias=NORM_EPS, scale=1.0 / NA_D)
                    S.recip(c[:, 0:n], c[:, 0:n], R=[c], W=[c])
                    S.stt("dve", dst[:, h, t0:t0 + n], a[:, 0:n], gain[:, 0:1], c[:, 0:n], ALU.mult, ALU.mult, R=[a, gain, c], W=[dst])
        vst = [S.sb("vst", [128, 1024]) for _ in range(2)]
        for tt in range(NT):
            v_ = vst[tt % 2]
            S.dma("sp", v_[:, :], self.pk[tt * 128:(tt + 1) * 128, 0:1024], R=[self.r_pk[tt]], W=[v_])
            S.copy("pool", V[:, tt, :], v_[:, :], R=[v_], W=[V])
        btf = [S.sb("btf", [128, 15, 64]) for _ in range(2)]
        for h in range(8):
            b_ = btf[h % 2]
            S.memset("pool", b_[:, :, :], 0.0, W=[b_])
            S.dma("sp", b_[:, 0:7, :], self.rpb[l, h, 0:896, :].rearrange("(m p) w -> p m w", p=128), W=[b_])
            S.dma("sp", b_[0:64, 7, :], self.rpb[l, h, 896:960, :], W=[b_])
            S.dma("sp", b_[:, 8:15, :], self.rpb[l, h, 64:960, :].rearrange("(m p) w -> p m w", p=128), W=[b_])
            S.tt("dve", btab[:, h, :, :], b_[:, :, :], self.C("na_mask").unsqueeze(1).to_broadcast([128, 15, 64]), ALU.add,
                 R=[b_, self.consts], W=[btab])
        S.barrier()
        S.release(mk2)
        PT = [S.sb("PT", [128, 512], BF16) for _ in range(3)]
        zt = [S.sb("zt", [128, 8, 64]) for _ in range(2)]
        rinv = [S.sb("rinv", [128, 64]) for _ in range(3)]
        tmpo = [S.sb("tmpo", [128, 64]) for _ in range(3)]
        kh = min(8, ROWS)
        blocks = [("lat", r) for r in range(ROWS)]
        if need_ctx:
            blocks += [("ctx", cb) for cb in range(CTX // 64)]
        cnt = 0
        items = []
        for bi, (kind, r) in enumerate(blocks):
            if kind == "lat":
                q0 = CTX + r * 64
                rs0 = min(max(r - kh // 2, 0), ROWS - kh)
                tiles = []
                for lt in range(rs0 // 2, (rs0 + kh - 1) // 2 + 1):
                    lo = 0 if 2 * lt >= rs0 else 64
                    hi = 128 if 2 * lt + 1 <= rs0 + kh - 1 else 64
                    ro0 = 2 * lt - r + 7
                    idx = ro0 // 2 if ro0 % 2 == 0 else 8 + (ro0 - 1) // 2
                    assert 0 <= idx < 15
                    tiles.append((NTC + lt, lo, hi, idx))
                for ct in range(NTC):
                    tiles.append((ct, 0, 128, None))
            else:
                q0 = r * 64
                tiles = [(ct, 0, 128, None) for ct in range(NTC)]
            for h in range(8):
                items.append((tiles, q0, zt[bi % 2], h))

        def stage1(it, n):
            tiles, q0, z_, h = it
            nk = len(tiles)
            if h == 0:
                S.dma("sp", z_[:, :, :], self.pT[16 * 128:24 * 128, q0:q0 + 64].rearrange("(h p) t -> p h t", p=128),
                      R=self.r_pT[16:24], W=[z_])
            ps_s = self.PB[n % 3]
            pt_ = PT[n % 3]
            for i, (tt, lo, hi, idx) in enumerate(tiles):
                S.mm(ps_s[:, i * 64:(i + 1) * 64], KTt[:, h, tt * 128:(tt + 1) * 128], QT[:, h, q0:q0 + 64], start=True, stop=(idx is None),
                     R=[KTt, QT], W=[ps_s])
                if idx is not None:
                    S.mm(ps_s[:, i * 64:(i + 1) * 64], self.CB("ident"), btab[:, h, idx, :], start=False, stop=True,
                         R=[btab, self.consts_bf], W=[ps_s])
            S.act(pt_[:, 0:nk * 64], ps_s[:, 0:nk * 64], AF.Exp, R=[ps_s], W=[pt_])
            for i, (tt, lo, hi, idx) in enumerate(tiles):
                if lo != 0:
                    S.memset("pool", pt_[0:lo, i * 64:(i + 1) * 64], 0.0, W=[pt_])
                if hi != 128:
                    S.memset("pool", pt_[hi:128, i * 64:(i + 1) * 64], 0.0, W=[pt_])

        def stage2(it, n):
            tiles, q0, z_, h = it
            nk = len(tiles)
            ps_o = self.PB[3 + n % 3]
            pt_ = PT[n % 3]
            ri, to = rinv[n % 3], tmpo[n % 3]
            for i, (tt, lo, hi, idx) in enumerate(tiles):
                S.mm(ps_o[:, 0:64], V[:, tt, h * 128:(h + 1) * 128], pt_[:, i * 64:(i + 1) * 64], start=(i == 0), stop=(i == nk - 1),
                     R=[V, pt_], W=[ps_o])
            for i, (tt, lo, hi, idx) in enumerate(tiles):
                S.mm(ps_o[:, 64:128], self.CB("ones"), pt_[:, i * 64:(i + 1) * 64], start=(i == 0), stop=(i == nk - 1),
                     R=[self.consts_bf, pt_], W=[ps_o])
            S.recip(ri[:, :], ps_o[:, 64:128], R=[ps_o], W=[ri])
            S.tt("dve", to[:, :], ps_o[:, 0:64], ri[:, :], ALU.mult, R=[ps_o, ri], W=[to])
            S.tt("pool", yTn[:, h, q0:q0 + 64], to[:, :], z_[:, h, :], ALU.mult, R=[to, z_], W=[yTn])

        for n, it in enumerate(items):
            stage1(it, n)
            if n > 0:
                stage2(items[n - 1], n - 1)
        stage2(items[-1], len(items) - 1)
        t_lo = 0 if need_ctx else CTX
        S.dma("sp", self.yT[0:1024, t_lo:T].rearrange("(h p) t -> p h t", p=128), yTn[:, :, t_lo:T], R=[yTn], W=self.r_yT[0:8])
        S.barrier()
        S.release(mk)

    def ring(self, name, n, shape, dtype=F32):
        bufs = [self.S.sb(name, shape, dtype) for _ in range(n)]
        state = {"i": 0}

        def get():
            b = bufs[state["i"] % n]
            state["i"] += 1
            return b
        return get

    def psum_slots(self, banks, width):
        slots = []
        per = 512 // width
        for q in range(per):
            for b in banks:
                slots.append((self.PB[b], q * width, self.PB[b].r))
        state = {"i": 0}

        def get():
            pb, c0, r = slots[state["i"] % len(slots)]
            state["i"] += 1
            return pb, c0, r
        return get

    def inv_doubling(self, M0, M0T, getbuf, getps, rconst, getfinal):
        S = self.S
        ident = self.CB("ident") if DN_BF16 else self.C("ident")
        P = getbuf()
        S.tt("pool", P[:, :], M0[:, :], ident, ALU.add, R=[M0, rconst], W=[P])
        M, MT = M0, M0T
        for k in range(6):
            newM = newMT = None
            if k <= 4:
                pb, c0, r = getps()
                S.mm(pb[:, c0:c0 + 128], MT[:, :], M[:, :], R=[MT, M], W=[r])
                newM = getbuf()
                S.copy("act", newM[:, :], pb[:, c0:c0 + 128], R=[r], W=[newM])
                pb, c0, r = getps()
                S.mm(pb[:, c0:c0 + 128], M[:, :], MT[:, :], R=[MT, M], W=[r])
                newMT = getbuf()
                S.copy("dve", newMT[:, :], pb[:, c0:c0 + 128], R=[r], W=[newMT])
            if k >= 1:
                pb, c0, r = getps()
                S.mm(pb[:, c0:c0 + 128], MT[:, :], P[:, :], R=[MT, P], W=[r])
                newP = getfinal() if k == 5 else getbuf()
                S.tt("dve", newP[:, :], pb[:, c0:c0 + 128], P[:, :], ALU.add, R=[r, P], W=[newP])
                P = newP
            if newM is not None:
                M, MT = newM, newMT
            yield
        self._inv_result = P

    def phase_dn(self, l):
        S, T, NT, NTC, CTX, SEQ = self.S, self.T, self.NT, self.NTC, self.CTX, self.SEQ
        mk = S.mark()
        cst = self.consts
        ab = S.sb("ab", [128, NT, 16])
        S.dma("sp", ab[:, :, :], self.pk[:, 1024:1040].rearrange("(tt p) c -> p tt c", p=128), R=self.r_pk, W=[ab])
        expA = S.sb("expA", [128, 8])
        dtb = S.sb("dtb", [128, 8])
        S.dma("sp", expA[:, :], self.dn_A[l].partition_broadcast(128), W=[expA])
        S.dma("sp", dtb[:, :], self.dn_dt[l].partition_broadcast(128), W=[dtb])
        S.act(expA[:, :], expA[:, :], AF.Exp, R=[expA], W=[expA])
        G = S.sb("G", [128, NT, 8])
        Bt = S.sb("Bt", [128, NT, 8])
        NB = S.sb("NB", [128, NT, 8])
        GC = S.sb("GC", [128, NT, 8])
        EG = S.sb("EG", [128, NT, 8])
        BG = S.sb("BG", [128, NT, 8])
        ED = S.sb("ED", [128, NT, 8])
        GL = S.sb("GL", [128, 2, NT, 8])
        S.tt("dve", G[:, :, :], ab[:, :, 0:8], dtb[:, :].unsqueeze(1).to_broadcast([128, NT, 8]), ALU.add, R=[ab, dtb], W=[G])
        S.act(G[:, :, :], G[:, :, :], AF.Exp, R=[G], W=[G])
        S.act(G[:, :, :], G[:, :, :], AF.Ln, R=[G], W=[G], bias=1.0)
        S.stt("dve", G[:, :, :], G[:, :, :], -1.0, expA[:, :].unsqueeze(1).to_broadcast([128, NT, 8]), ALU.mult, ALU.mult, R=[G, expA], W=[G])
        S.act(Bt[:, :, :], ab[:, :, 8:16], AF.Sigmoid, R=[ab], W=[Bt])
        S.ts("dve", NB[:, :, :], Bt[:, :, :], -1.0, None, ALU.mult, R=[Bt], W=[NB])
        pb = self.PB[0]
        for d, tri in ((0, "tri_f"), (1, "tri_b")):
            S.mm(pb[:, d * NT * 4:(d + 1) * NT * 4], self.C(tri), G[:, :, d * 4:(d + 1) * 4], R=[G, cst], W=[pb])
        S.copy("dve", GC[:, :, 0:4], pb[:, 0:NT * 4].rearrange("p (t c) -> p t c", c=4), R=[pb], W=[GC])
        S.copy("dve", GC[:, :, 4:8], pb[:, NT * 4:NT * 8].rearrange("p (t c) -> p t c", c=4), R=[pb], W=[GC])
        pb = self.PB[1]
        S.mm(pb[:, 0:NT * 8], self.C("blk"), G[:, :, :], R=[G, cst], W=[pb])
        S.tt("dve", ED[:, :, :], pb[:, 0:NT * 8].rearrange("p (t c) -> p t c", c=8), GC[:, :, :], ALU.subtract, R=[pb, GC], W=[ED])
        S.act(ED[:, :, :], ED[:, :, :], AF.Exp, R=[ED], W=[ED])
        S.act(EG[:, :, :], GC[:, :, :], AF.Exp, R=[GC], W=[EG])
        S.tt("dve", BG[:, :, :], EG[:, :, :], Bt[:, :, :], ALU.mult, R=[EG, Bt], W=[BG])
        pb = self.PB[2]
        for j in range(2):
            S.mm(pb[:, j * NT * 8:(j + 1) * NT * 8], self.C("ind%d" % j), G[:, :, :], R=[G, cst], W=[pb])
        S.act(GL[:, :, :, :], pb[:, 0:2 * NT * 8].rearrange("p (j t c) -> p j t c", j=2, c=8), AF.Exp, R=[pb], W=[GL])
        import os
        CUT = int(os.environ.get("DN_CUT", "99"))
        if CUT == 0:
            S.barrier(); S.release(mk); return
        cw = S.sb("cw", [128, 12, 5])
        for j in range(5):
            S.dma("sp", cw[:, :, j], self.dn_conv[l, j].rearrange("(t p) -> p t", p=128), W=[cw], slow=True)
        ng = S.sb("ng", [128, 1])
        S.dma("sp", ng[:, :], self.dn_norm[l].rearrange("(p o) -> p o", o=1), W=[ng])
        getps = [self.psum_slots([0, 1, 2, 3], 128), self.psum_slots([4, 5, 6, 7], 128)]
        W_ = CTX + 4 + SEQ
        qkv = [S.sb("qkvT", [128, T]) for _ in range(3)]
        qkb = [S.sb("qkb", [128, T], BF16) for _ in range(2)]
        cnt = 0
        for h in range(DN_H):
            mk1 = S.mark()
            rope = S.sb("rope", [128, 2, SEQ])
            S.dma("sp", rope[:, :, :], self.rope_in, W=[rope])
            rawp = [S.sb("rawp", [128, W_ + 4]) for _ in range(2)]
            for r_ in rawp:
                S.memset("pool", r_[:, :], 0.0, W=[r_])
            acc = [S.sb("acc", [128, W_]) for _ in range(2)]
            xs_ = S.sb("xs_", [128, T])
            sq = S.sb("sq", [128, T])
            rin = [S.sb("rin", [128, 512]) for _ in range(2)]
            t1 = [S.sb("t1", [128, 512]) for _ in range(2)]
            t2 = [S.sb("t2", [128, 512]) for _ in range(2)]
            for s3 in range(3):
                ct = s3 * 4 + h
                fi = 24 + ct
                rp, ac = rawp[cnt % 2], acc[cnt % 2]
                eng = "dve"
                cnt += 1
                S.dma("sp", rp[:, 2:2 + CTX], self.pT[fi * 128:(fi + 1) * 128, 0:CTX], R=[self.r_pT[fi]], W=[rp])
                S.dma("sp", rp[:, CTX + 6:CTX + 6 + SEQ], self.pT[fi * 128:(fi + 1) * 128, CTX:T], R=[self.r_pT[fi]], W=[rp])
                S.ts(eng, ac[:, :], rp[:, 0:W_], cw[:, ct, 0:1], None, ALU.mult, R=[rp, cw], W=[ac])
                for j in range(1, 5):
                    S.stt(eng, ac[:, :], rp[:, j:j + W_], cw[:, ct, j:j + 1], ac[:, :], ALU.mult, ALU.add, R=[rp, cw, ac], W=[ac])
                dst = qkv[s3]
                if s3 == 2:
                    S.act(dst[:, 0:CTX], ac[:, 0:CTX], AF.Silu, R=[ac], W=[dst])
                    S.act(dst[:, CTX:T], ac[:, CTX + 4:W_], AF.Silu, R=[ac], W=[dst])
                    continue
                S.act(xs_[:, 0:CTX], ac[:, 0:CTX], AF.Silu, R=[ac], W=[xs_])
                S.act(xs_[:, CTX:T], ac[:, CTX + 4:W_], AF.Silu, R=[ac], W=[xs_])
                S.act(sq[:, :], xs_[:, :], AF.Square, R=[xs_], W=[sq])
                scale = float(DN_D) ** -0.5 if s3 == 0 else 1.0
                for bi, t0 in enumerate(range(0, T, 512)):
                    n = min(512, T - t0)
                    pb = self.PB[bi % 2]
                    ri = rin[bi % 2]
                    S.mm(pb[:, 0:n], self.C("ones"), sq[:, t0:t0 + n], R=[sq, cst], W=[pb])
                    S.act(ri[:, 0:n], pb[:, 0:n], AF.Sqrt, R=[pb], W=[ri], bias=NORM_EPS, scale=1.0)
                    S.recip(ri[:, 0:n], ri[:, 0:n], R=[ri], W=[ri])
                    S.stt("dve", xs_[:, t0:t0 + n], xs_[:, t0:t0 + n], scale, ri[:, 0:n], ALU.mult, ALU.mult, R=[xs_, ri], W=[xs_])
                S.copy("pool", dst[:, 0:CTX], xs_[:, 0:CTX], R=[xs_], W=[dst])
                for bi, t0 in enumerate(range(0, SEQ, 512)):
                    n = min(512, SEQ - t0)
                    pb = self.PB[2 + bi % 2]
                    a_, b_ = t1[bi % 2], t2[bi % 2]
                    S.mm(pb[:, 0:n], self.C("Rt"), xs_[:, CTX + t0:CTX + t0 + n], R=[xs_, cst], W=[pb])
                    S.tt("pool", a_[:, 0:n], xs_[:, CTX + t0:CTX + t0 + n], rope[:, 0, t0:t0 + n], ALU.mult, R=[xs_, rope], W=[a_])
                    S.tt("dve", b_[:, 0:n], pb[:, 0:n], rope[:, 1, t0:t0 + n], ALU.mult, R=[pb, rope], W=[b_])
                    S.tt("pool", dst[:, CTX + t0:CTX + t0 + n], a_[:, 0:n], b_[:, 0:n], ALU.add, R=[a_, b_], W=[dst])
            qT, kT, vT = qkv
            qTb, kTb = qkb
            S.copy("pool", qTb[:, :], qT[:, :], R=[qT], W=[qTb])
            S.copy("act", kTb[:, :], kT[:, :], R=[kT], W=[kTb])
            S.barrier()
            S.release(mk1)
            if CUT == 1:
                S.release(mk); return
            mk2 = S.mark()
            oT = [S.sb("oT", [128, T]) for _ in range(2)]
            Sst = [S.sb("Sst", [128, 128]) for _ in range(2)]
            getbuf = [self.ring("dnb%d" % d, 56, [128, 128]) for d in range(2)]
            gethb = [self.ring("dnh%d" % d, 20, [128, 128], BF16 if DN_BF16 else F32) for d in range(2)]
            zt = S.sb("zt", [128, T])
            sq = S.sb("sq", [128, T])
            rin = [S.sb("rin", [128, 512]) for _ in range(2)]
            for d in range(2):
                S.memset("pool", Sst[d][:, :], 0.0, W=[Sst[d]])

            handoff = [[], []]
            doneB = [0, 0]

            def dn_order(d):
                if d == 0:
                    return list(range(NT))
                return list(range(NTC - 1, -1, -1)) + list(range(NT - 1, NTC - 1, -1))

            def genA(d):
                gb, gp, gh = getbuf[d], getps[d], gethb[d]
                identb = self.CB("ident")
                col = d * 4 + h
                order = dn_order(d)
                if d == 0:
                    PMn, NMn = "PM_LI", "NM_UI"
                else:
                    PMn, NMn = "PM_UI", "NM_LI"
                ident = self.C("ident")
                for ti, tt in enumerate(order):
                    while ti - doneB[d] >= 2:
                        yield
                    tok = slice(tt * 128, (tt + 1) * 128)
                    gc = GC[:, tt, col:col + 1]
                    dg = gb()
                    S.ts("dve", dg[:, :], ident, gc, None, ALU.mult, R=[cst, GC], W=[dg])
                    pg, cg, rg = gp()
                    S.mm(pg[:, cg:cg + 128], self.C("ones"), dg[:, :], R=[dg, cst], W=[rg])
                    tD, Dm, tDT, DT, EGr = gb(), gb(), gb(), gb(), gb()
                    S.stt("dve", tD[:, :], pg[:, cg:cg + 128], gc, self.C(PMn), ALU.subtract, ALU.add, R=[rg, GC, cst], W=[tD])
                    S.act(Dm[:, :], tD[:, :], AF.Exp, R=[tD], W=[Dm], scale=-1.0)
                    S.stt("dve", tDT[:, :], pg[:, cg:cg + 128], gc, self.C(NMn), ALU.subtract, ALU.add, R=[rg, GC, cst], W=[tDT])
                    S.act(DT[:, :], tDT[:, :], AF.Exp, R=[tDT], W=[DT])
                    S.act(EGr[:, :], pg[:, cg:cg + 128], AF.Exp, R=[rg], W=[EGr])
                    yield
                    if CUT == 2:
                        continue
                    pk_, ck, rk = gp()
                    kX, qX = (kTb, qTb) if DN_BF16 else (kT, qT)
                    S.mm(pk_[:, ck:ck + 128], kX[:, tok], kX[:, tok], R=[kX], W=[rk])
                    pq, cq, rq = gp()
                    S.mm(pq[:, cq:cq + 128], kX[:, tok], qX[:, tok], R=[kX, qX], W=[rq])
                    nL, M0T, attnT = gb(), gh(), gb()
                    S.stt("dve", nL[:, :], pk_[:, ck:ck + 128], NB[:, tt, col:col + 1], Dm[:, :], ALU.mult, ALU.mult, R=[rk, NB, Dm], W=[nL])
                    S.tt("pool", M0T[:, :], nL[:, :], self.C("OFFD"), ALU.mult, R=[nL, cst], W=[M0T])
                    S.tt("dve", attnT[:, :], pq[:, cq:cq + 128], DT[:, :], ALU.mult, R=[rq, DT], W=[attnT])
                    pt_, c_t, rt = gp()
                    if DN_BF16:
                        ptb = pt_[:, :].bitcast(BF16)[:, 2 * c_t:2 * c_t + 128]
                        S.tr(ptb, M0T[:, :], identb, R=[M0T, self.consts_bf], W=[rt])
                    else:
                        ptb = pt_[:, c_t:c_t + 128]
                        S.tr(ptb, M0T[:, :], ident, R=[M0T, cst], W=[rt])
                    M0 = gh()
                    S.copy("act", M0[:, :], ptb, R=[rt], W=[M0])
                    yield
                    if CUT == 3:
                        continue
                    yield from self.inv_doubling(M0, M0T, gh, gp, self.consts_bf, gb)
                    TT_ = self._inv_result
                    if CUT == 4:
                        continue
                    pkt, ckt, rkt = gp()
                    S.tr(pkt[:, ckt:ckt + 128], kT[:, tok], ident, R=[kT, cst], W=[rkt])
                    kbg, kdec, vb = gb(), gb(), gb()
                    S.ts("dve", kbg[:, :], pkt[:, ckt:ckt + 128], BG[:, tt, col:col + 1], None, ALU.mult, R=[rkt, BG], W=[kbg])
                    S.ts("dve", kdec[:, :], pkt[:, ckt:ckt + 128], ED[:, tt, col:col + 1], None, ALU.mult, R=[rkt, ED], W=[kdec])
                    pvt, cvt, rvt = gp()
                    S.tr(pvt[:, cvt:cvt + 128], vT[:, tok], ident, R=[vT, cst], W=[rvt])
                    S.ts("dve", vb[:, :], pvt[:, cvt:cvt + 128], Bt[:, tt, col:col + 1], None, ALU.mult, R=[rvt, Bt], W=[vb])
                    yield
                    pu, cu, ru = gp()
                    S.mm(pu[:, cu:cu + 128], TT_[:, :], vb[:, :], R=[TT_, vb], W=[ru])
                    U0, nWT, qdT = gb(), gb(), gb()
                    S.copy("act", U0[:, :], pu[:, cu:cu + 128], R=[ru], W=[U0])
                    pw, cw_, rw = gp()
                    S.mm(pw[:, cw_:cw_ + 128], kbg[:, :], TT_[:, :], R=[TT_, kbg], W=[rw])
                    S.ts("dve", nWT[:, :], pw[:, cw_:cw_ + 128], -1.0, None, ALU.mult, R=[rw], W=[nWT])
                    S.tt("pool", qdT[:, :], qT[:, tok], EGr[:, :], ALU.mult, R=[qT, EGr], W=[qdT])
                    handoff[d].append((tt, attnT, U0, nWT, qdT, kdec))
                    yield

            def genB(d):
                gb, gp = getbuf[d], getps[d]
                col = d * 4 + h
                st = Sst[d]
                for ti, tt in enumerate(dn_order(d)):
                    while not handoff[d]:
                        yield
                    tt_, attnT, U0, nWT, qdT, kdec = handoff[d].pop(0)
                    assert tt_ == tt
                    vnew = gb()
                    for j in ((0, 1) if d == 0 else (1, 0)):
                        cj = slice(64 * j, 64 * j + 64)
                        pv, cv, rv = gp()
                        S.mm(pv[:, cv:cv + 128], nWT[:, :], st[:, :], R=[nWT, st], W=[rv])
                        S.tt("dve", vnew[cj, :], U0[cj, :], pv[cj, cv:cv + 128], ALU.add, R=[U0, rv], W=[vnew])
                        po, co, ro = gp()
                        S.mm(po[:, co:co + 64], st[:, :], qdT[:, cj], start=True, stop=False, R=[st, qdT], W=[ro])
                        S.mm(po[:, co:co + 64], vnew[cj, :], attnT[cj, cj], start=False, stop=True, R=[vnew, attnT], W=[ro])
                        S.copy("act", oT[d][:, tt * 128 + 64 * j:tt * 128 + 64 * j + 64], po[:, co:co + 64], R=[ro], W=[oT[d]])
                        ps2, c2, r2 = gp()
                        S.mm(ps2[:, c2:c2 + 128], kdec[cj, :], vnew[cj, :], R=[kdec, vnew], W=[r2])
                        S.stt("dve", st[:, :], st[:, :], GL[:, j, tt, col:col + 1], ps2[:, c2:c2 + 128], ALU.mult, ALU.add, R=[st, GL, r2], W=[st])
                        yield
                    doneB[d] += 1

            interleave([genA(0), genB(0), genA(1), genB(1)], lead=[6, 0, 0, 0])
            fi = 36 + h
            S.dma("sp", zt[:, :], self.pT[fi * 128:(fi + 1) * 128, :], R=[self.r_pT[fi]], W=[zt])
            S.tt("pool", oT[0][:, :], oT[0][:, :], oT[1][:, :], ALU.add, R=[oT[0], oT[1]], W=[oT[0]])
            S.act(sq[:, :], oT[0][:, :], AF.Square, R=[oT[0]], W=[sq])
            yb = S.sb("yb", [128, T], BF16)
            for bi, t0 in enumerate(range(0, T, 512)):
                n = min(512, T - t0)
                pb = self.PB[bi % 2]
                ri = rin[bi % 2]
                S.mm(pb[:, 0:n], self.C("ones"), sq[:, t0:t0 + n], R=[sq, cst], W=[pb])
                S.act(ri[:, 0:n], pb[:, 0:n], AF.Sqrt, R=[pb], W=[ri], bias=NORM_EPS, scale=1.0 / DN_D)
                S.recip(ri[:, 0:n], ri[:, 0:n], R=[ri], W=[ri])
                S.stt("dve", ri[:, 0:n], oT[0][:, t0:t0 + n], ng[:, 0:1], ri[:, 0:n], ALU.mult, ALU.mult, R=[oT[0], ng, ri], W=[ri])
                S.tt("dve", yb[:, t0:t0 + n], ri[:, 0:n], zt[:, t0:t0 + n], ALU.mult, R=[ri, zt], W=[yb])
            S.dma("sp", self.yT[(8 + h) * 128:(9 + h) * 128, :], yb[:, :], R=[yb], W=[self.r_yT[8 + h]])
            S.barrier()
            S.release(mk2)
        S.release(mk)

    def inv_doubling_b(self, M0, M0T, getbuf, getbank, rconst, getfinal):
        S = self.S
        ident = self.C("ident")
        P = getbuf()
        S.tt("dve", P[:, :, :], M0[:, :, :], ident.unsqueeze(1).to_broadcast([128, 4, 128]), ALU.add, R=[M0, rconst], W=[P])
        M, MT = M0, M0T
        for k in range(6):
            newM = newMT = None
            if k <= 4:
                pb = getbank()
                for hh in range(4):
                    S.mm(pb[:, hh * 128:(hh + 1) * 128], MT[:, hh, :], M[:, hh, :], R=[MT, M], W=[pb])
                newM = getbuf()
                S.copy("act", newM[:, :, :], pb[:, :].rearrange("p (h c) -> p h c", c=128), R=[pb], W=[newM])
                pb = getbank()
                for hh in range(4):
                    S.mm(pb[:, hh * 128:(hh + 1) * 128], M[:, hh, :], MT[:, hh, :], R=[MT, M], W=[pb])
                newMT = getbuf()
                S.copy("dve", newMT[:, :, :], pb[:, :].rearrange("p (h c) -> p h c", c=128), R=[pb], W=[newMT])
            if k >= 1:
                pb = getbank()
                for hh in range(4):
                    S.mm(pb[:, hh * 128:(hh + 1) * 128], MT[:, hh, :], P[:, hh, :], R=[MT, P], W=[pb])
                newP = getfinal() if k == 5 else getbuf()
                S.tt("dve", newP[:, :, :], pb[:, :].rearrange("p (h c) -> p h c", c=128), P[:, :, :], ALU.add, R=[pb, P], W=[newP])
                P = newP
            if newM is not None:
                M, MT = newM, newMT
            yield
        self._inv_result_b = P

    def phase_rw(self, l):
        S, T, NT, NTC, CTX, SEQ = self.S, self.T, self.NT, self.NTC, self.CTX, self.SEQ
        mk = S.mark()
        cst = self.consts
        C1 = -float(np.exp(-0.5))
        ro_d = [self.nc.dram_tensor("ro%d_%d" % (l, d), [T, 512], F32).ap() for d in range(2)]
        r_ro = [[Res("ro%d_%d" % (d, t)) for t in range(NT)] for d in range(2)]
        import os
        RWCUT = int(os.environ.get("RW_CUT", "99"))
        BCUT = int(os.environ.get("RW_BCUT", "99"))
        HG = 4
        CW = HG * 64
        XW = 4 * CW + 192
        ident = self.C("ident")
        for hg in range(2):
            mkg = S.mark()
            c0 = hg * CW
            mu = S.sb("mu", [128, XW])
            S.dma("sp", mu[:, 0:4 * CW].rearrange("p (s c) -> p s c", c=CW),
                  self.rw_mu[l, 0:2048].rearrange("(s c) -> s c", c=512)[:, c0:c0 + CW].partition_broadcast(128), W=[mu])
            S.dma("sp", mu[:, 4 * CW:XW], self.rw_mu[l, 2048:2240].partition_broadcast(128), W=[mu])

            def bc(name, src):
                t_ = S.sb(name, [128, CW])
                S.dma("sp", t_[:, :], src.partition_broadcast(128), W=[t_])
                return t_
            w0 = [bc("w0", self.rw_w0[l, d, c0:c0 + CW]) for d in range(2)]
            a0 = [bc("a0", self.rw_a0[l, d, c0:c0 + CW]) for d in range(2)]
            k_k = bc("k_k", self.rw_k_k[l, c0:c0 + CW])
            k_a = bc("k_a", self.rw_k_a[l, c0:c0 + CW])
            r_k = bc("r_k", self.rw_r_k[l, c0:c0 + CW])
            ln_w = bc("ln_w", self.rw_ln_w[l, c0:c0 + CW])
            ln_b = bc("ln_b", self.rw_ln_b[l, c0:c0 + CW])
            wup = S.sb("wup", [96, 2, CW])
            aup = S.sb("aup", [96, 2, CW])
            for d in range(2):
                S.dma("sp", wup[:, d, :], self.rw_w_up[l, d, :, c0:c0 + CW], W=[wup])
                S.dma("sp", aup[:, d, :], self.rw_a_up[l, d, :, c0:c0 + CW], W=[aup])
            Sst = [S.sb("Sst", [64, HG, 64]) for _ in range(2)]
            for d in range(2):
                S.memset("pool", Sst[d][:, :, :], 0.0, W=[Sst[d]])

            def stream(d):
                st = Sst[d]
                banks = [0, 1, 2, 3] if d == 0 else [4, 5, 6, 7]
                bstate = {"i": 0}

                def gbank():
                    b = self.PB[banks[bstate["i"] % 4]]
                    bstate["i"] += 1
                    return b
                curs = [S.sb("cur", [128, XW]) for _ in range(2)]
                prv = S.sb("prv", [128, XW])
                nxt = S.sb("nxt", [128, XW])
                g2 = self.ring("g2_%d" % d, 22, [128, CW])
                g3 = self.ring("g3_%d" % d, 9, [128, HG, 128])
                g3f = self.ring("g3f_%d" % d, 1, [128, HG, 128])
                gW = self.ring("gW_%d" % d, 1, [64, HG, 128])
                g3k = self.ring("g3k_%d" % d, 3, [128, HG, 128])
                gT = self.ring("gT_%d" % d, 4, [64, HG, 128])
                gR = self.ring("gR_%d" % d, 1, [64, HG, 128])
                sm = self.ring("sm_%d" % d, 10, [128, HG])
                if d == 0:
                    order = list(range(NT))
                    tri, LSn, USn, LIn, UIn = "tri_f", "LS", "US", "LI", "UI"
                else:
                    order = list(range(NTC - 1, -1, -1)) + list(range(NT - 1, NTC - 1, -1))
                    tri, LSn, USn, LIn, UIn = "tri_b", "US", "LS", "UI", "LI"

                def mb(name):
                    return self.C(name).unsqueeze(1).to_broadcast([128, HG, 128])

                def v3(ap):
                    return ap.rearrange("p (h n) -> p h n", n=64)

                def issue_loads(tt, cur):
                    g0 = tt * 128
                    first = tt == 0 or tt == NTC
                    lastt = tt == NTC - 1 or tt == NT - 1

                    def ld(dst, r0, r1, p0):
                        n = r1 - r0
                        S.dma("sp", dst[p0:p0 + n, 0:4 * CW].rearrange("p (s c) -> p s c", c=CW),
                              self.pk[r0:r1, 1040:1040 + 2048].rearrange("t (s c) -> t s c", c=512)[:, :, c0:c0 + CW],
                              R=self.r_pk[max(tt - 1, 0):min(tt + 2, NT)], W=[dst])
                        S.dma("sp", dst[p0:p0 + n, 4 * CW:XW], self.pk[r0:r1, 1040 + 2048:1040 + 2240],
                              R=self.r_pk[max(tt - 1, 0):min(tt + 2, NT)], W=[dst])
                    ld(cur, g0, g0 + 128, 0)
                    if first:
                        S.memset("pool", prv[:, :], 0.0, W=[prv])
                        ld(prv, g0, g0 + 127, 1)
                    else:
                        ld(prv, g0 - 1, g0 + 127, 0)
                    if lastt:
                        S.memset("pool", nxt[:, :], 0.0, W=[nxt])
                        ld(nxt, g0 + 1, g0 + 128, 0)
                    else:
                        ld(nxt, g0 + 1, g0 + 129, 0)

                issue_loads(order[0], curs[0])
                for ti, tt in enumerate(order):
                    g0 = tt * 128
                    cur = curs[ti % 2]
                    S.tt("pool", prv[:, :], prv[:, :], nxt[:, :], ALU.add, R=[prv, nxt], W=[prv])
                    S.stt("dve", prv[:, :], prv[:, :], 0.5, cur[:, :], ALU.mult, ALU.subtract, R=[prv, cur], W=[prv])
                    S.tt("pool", prv[:, :], prv[:, :], mu[:, :], ALU.mult, R=[prv, mu], W=[prv])
                    S.tt("dve", cur[:, :], cur[:, :], prv[:, :], ALU.add, R=[cur, prv], W=[cur])
                    xr, xk, xv, xz = (cur[:, i * CW:(i + 1) * CW] for i in range(4))
                    xwd, xad = cur[:, 4 * CW:4 * CW + 96], cur[:, 4 * CW + 96:XW]
                    if ti + 1 < len(order):
                        issue_loads(order[ti + 1], curs[(ti + 1) % 2])
                    yield
                    if RWCUT == 0:
                        continue
                    th = g2()
                    S.act(th[:, 0:96], xwd, AF.Tanh, R=[cur], W=[th])
                    pb = gbank()
                    S.tr(pb[0:96, 0:128], th[:, 0:96], ident, R=[th, cst], W=[pb])
                    S.tr(pb[0:96, 128:256], xad, ident, R=[cur, cst], W=[pb])
                    lT = g2()
                    S.copy("act", lT[0:96, 0:256], pb[0:96, 0:256], R=[pb], W=[lT])
                    pbw = gbank()
                    S.mm(pbw[:, 0:CW], lT[0:96, 0:128], wup[:, d, :], R=[lT, wup], W=[pbw])
                    S.mm(pbw[:, CW:2 * CW], lT[0:96, 128:256], aup[:, d, :], R=[lT, aup], W=[pbw])
                    sg, a_ = g2(), g2()
                    S.tt("dve", sg[:, :], pbw[:, 0:CW], w0[d][:, :], ALU.add, R=[pbw, w0[d]], W=[sg])
                    S.act(sg[:, :], sg[:, :], AF.Sigmoid, R=[sg], W=[sg])
                    S.tt("dve", a_[:, :], pbw[:, CW:2 * CW], a0[d][:, :], ALU.add, R=[pbw, a0[d]], W=[a_])
                    S.act(a_[:, :], a_[:, :], AF.Sigmoid, R=[a_], W=[a_])
                    yield
                    if RWCUT == 1:
                        continue
                    kk, sq_ = g2(), g2()
                    ss, ss2 = sm(), sm()
                    S.tt("pool", kk[:, :], xk, k_k[:, :], ALU.mult, R=[cur, k_k], W=[kk])
                    S.act(sq_[:, :], kk[:, :], AF.Square, R=[kk], W=[sq_])
                    S.op("dve", lambda e, o=ss, i=sq_: e.tensor_reduce(out=o[:, :], in_=v3(i[:, :]), axis=AX.X, op=ALU.add), R=[sq_], W=[ss])
                    S.act(ss[:, :], ss[:, :], AF.Sqrt, R=[ss], W=[ss], bias=NORM_EPS, scale=1.0)
                    S.recip(ss[:, :], ss[:, :], R=[ss], W=[ss])
                    S.tt("dve", v3(kk[:, :]), v3(kk[:, :]), ss[:, :].unsqueeze(2).to_broadcast([128, HG, 64]), ALU.mult, R=[kk, ss], W=[kk])
                    kd, b_, rk = g2(), g2(), g2()
                    S.stt("dve", kd[:, :], a_[:, :], -1.0, k_a[:, :], ALU.add, ALU.mult, R=[a_, k_a], W=[kd])
                    S.stt("dve", kd[:, :], kd[:, :], 1.0, xk, ALU.add, ALU.mult, R=[kd, cur], W=[kd])
                    S.tt("pool", b_[:, :], kk[:, :], a_[:, :], ALU.mult, R=[kk, a_], W=[b_])
                    S.tt("pool", rk[:, :], xr, r_k[:, :], ALU.mult, R=[cur, r_k], W=[rk])
                    S.tt("pool", rk[:, :], rk[:, :], kd[:, :], ALU.mult, R=[rk, kd], W=[rk])
                    S.op("dve", lambda e, o=ss2, i=rk: e.tensor_reduce(out=o[:, :], in_=v3(i[:, :]), axis=AX.X, op=ALU.add), R=[rk], W=[ss2])
                    yield
                    if RWCUT == 2:
                        continue
                    pbc = gbank()
                    S.mm(pbc[:, 0:CW], self.C(tri), sg[:, :], R=[sg, cst], W=[pbc])
                    S.mm(pbc[:, CW:2 * CW], self.C("blk"), sg[:, :], R=[sg, cst], W=[pbc])
                    cum, tmp, rt_, kt_, bt_, at_, Kd, nBd = g2(), g2(), g2(), g2(), g2(), g2(), g2(), g2()
                    S.copy("act", cum[:, :], pbc[:, 0:CW], R=[pbc], W=[cum])
                    S.act(tmp[:, :], pbc[:, 0:CW], AF.Exp, R=[pbc], W=[tmp], scale=C1)
                    S.tt("dve", rt_[:, :], xr, tmp[:, :], ALU.mult, R=[cur, tmp], W=[rt_])
                    tmp = g2()
                    S.act(tmp[:, :], pbc[:, 0:CW], AF.Exp, R=[pbc], W=[tmp], scale=-C1)
                    S.tt("dve", kt_[:, :], kd[:, :], tmp[:, :], ALU.mult, R=[kd, tmp], W=[kt_])
                    S.tt("pool", bt_[:, :], b_[:, :], tmp[:, :], ALU.mult, R=[b_, tmp], W=[bt_])
                    tmp = g2()
                    S.tt("dve", tmp[:, :], cum[:, :], sg[:, :], ALU.subtract, R=[cum, sg], W=[tmp])
                    S.act(tmp[:, :], tmp[:, :], AF.Exp, R=[tmp], W=[tmp], scale=C1)
                    S.tt("pool", at_[:, :], kk[:, :], tmp[:, :], ALU.mult, R=[kk, tmp], W=[at_])
                    tmp = g2()
                    S.tt("dve", tmp[:, :], pbc[:, CW:2 * CW], cum[:, :], ALU.subtract, R=[pbc, cum], W=[tmp])
                    S.act(tmp[:, :], tmp[:, :], AF.Exp, R=[tmp], W=[tmp], scale=C1)
                    S.tt("dve", Kd[:, :], kd[:, :], tmp[:, :], ALU.mult, R=[kd, tmp], W=[Kd])
                    S.stt("dve", nBd[:, :], b_[:, :], -1.0, tmp[:, :], ALU.mult, ALU.mult, R=[b_, tmp], W=[nBd])
                    pbp = gbank()
                    for hh in range(HG):
                        S.mm(pbp[0:64, hh * 2:hh * 2 + 2], sg[:, hh * 64:(hh + 1) * 64], self.C("ind2"), R=[sg, cst], W=[pbp])
                    PCc = g2()
                    S.act(PCc[0:64, 0:2 * HG], pbp[0:64, 0:2 * HG], AF.Exp, R=[pbp], W=[PCc], scale=C1)
                    yield
                    if RWCUT == 3:
                        continue
                    XT = []
                    for src in (at_, bt_, kt_, rt_):
                        pbt = gbank()
                        for hh in range(HG):
                            S.tr(pbt[0:64, hh * 128:(hh + 1) * 128], src[:, hh * 64:(hh + 1) * 64], ident, R=[src, cst], W=[pbt])
                        x_ = gT()
                        S.copy("act" if len(XT) % 2 == 0 else "dve", x_[:, :, :], pbt[0:64, :].rearrange("p (h c) -> p h c", c=128), R=[pbt], W=[x_])
                        XT.append(x_)
                    aT, bT, kT, rT = XT
                    rTf = rT
                    yield
                    if RWCUT == 4:
                        continue
                    def scores(lh, rh, mask, neg, keep=False):
                        pbs = gbank()
                        for hh in range(HG):
                            S.mm(pbs[:, hh * 128:(hh + 1) * 128], lh[:, hh, :], rh[:, hh, :], R=[lh, rh], W=[pbs])
                        o_ = g3k() if keep else g3()
                        pv_ = pbs[:, :].rearrange("p (h c) -> p h c", c=128)
                        if neg:
                            S.stt("dve", o_[:, :, :], pv_, -1.0, mb(mask), ALU.mult, ALU.mult, R=[pbs, cst], W=[o_])
                        else:
                            S.tt("dve", o_[:, :, :], pv_, mb(mask), ALU.mult, R=[pbs, cst], W=[o_])
                        return o_
                    M0T = scores(aT, bT, LSn, True)
                    M0 = scores(bT, aT, USn, True)
                    AakT = scores(kT, aT, USn, False, True)
                    yield
                    if RWCUT == 5:
                        continue
                    ArkT = scores(kT, rT, UIn, False, True)
                    nArbT = scores(bT, rT, UIn, True, True)
                    yield
                    if RWCUT == 6:
                        continue
                    yield from self.inv_doubling_b(M0, M0T, g3, gbank, cst, g3f)
                    TT_ = self._inv_result_b
                    pbx = gbank()
                    for hh in range(HG):
                        S.mm(pbx[:, hh * 64:(hh + 1) * 64], AakT[:, hh, :], cur[:, 2 * CW + hh * 64:2 * CW + (hh + 1) * 64], R=[AakT, cur], W=[pbx])
                    X_ = g2()
                    S.copy("act", X_[:, :], pbx[:, 0:CW], R=[pbx], W=[X_])
                    pbu = gbank()
                    for hh in range(HG):
                        S.mm(pbu[:, hh * 64:(hh + 1) * 64], TT_[:, hh, :], X_[:, hh * 64:(hh + 1) * 64], R=[TT_, X_], W=[pbu])
                    U0, U_ = g2(), g2()
                    S.copy("dve", U0[:, :], pbu[:, 0:CW], R=[pbu], W=[U0])
                    pbw2 = gbank()
                    for hh in range(HG):
                        S.mm(pbw2[0:64, hh * 128:(hh + 1) * 128], at_[:, hh * 64:(hh + 1) * 64], TT_[:, hh, :], R=[at_, TT_], W=[pbw2])
                    WT = gW()
                    S.copy("act", WT[:, :, :], pbw2[0:64, :].rearrange("p (h c) -> p h c", c=128), R=[pbw2], W=[WT])
                    otok = g2()
                    yield
                    if RWCUT == 7:
                        continue
                    for j in ((0, 1) if d == 0 else (1, 0)):
                        cj = slice(64 * j, 64 * j + 64)
                        pb1 = gbank()
                        for hh in range(HG):
                            S.mm(pb1[:, hh * 64:(hh + 1) * 64], WT[:, hh, :], st[:, hh, :], R=[WT, st], W=[pb1])
                        S.tt("dve", U_[cj, :], U0[cj, :], pb1[cj, 0:CW], ALU.add, R=[U0, pb1], W=[U_])
                        if BCUT == 1:
                            continue
                        pb2a = gbank()
                        for hh in range(HG):
                            hs = slice(hh * 64, (hh + 1) * 64)
                            S.mm(pb2a[:, hs], rTf[:, hh, :], st[:, hh, :], R=[rTf, st], W=[pb2a])
                        pb2 = gbank()
                        for hh in range(HG):
                            hs = slice(hh * 64, (hh + 1) * 64)
                            S.mm(pb2[:, hs], ArkT[cj, hh, :], cur[cj, 2 * CW + hh * 64:2 * CW + (hh + 1) * 64], R=[ArkT, cur], W=[pb2])
                        pb2c = gbank()
                        for hh in range(HG):
                            hs = slice(hh * 64, (hh + 1) * 64)
                            S.mm(pb2c[:, hs], nArbT[cj, hh, :], U_[cj, hs], R=[nArbT, U_], W=[pb2c])
                        S.copy("act", otok[cj, :], pb2a[cj, 0:CW], R=[pb2a], W=[otok])
                        S.tt("dve", otok[cj, :], otok[cj, :], pb2[cj, 0:CW], ALU.add, R=[otok, pb2], W=[otok])
                        S.tt("dve", otok[cj, :], otok[cj, :], pb2c[cj, 0:CW], ALU.add, R=[otok, pb2c], W=[otok])
                        if BCUT == 3:
                            continue
                        pb3 = gbank()
                        for hh in range(HG):
                            hs = slice(hh * 64, (hh + 1) * 64)
                            S.mm(pb3[0:64, hs], Kd[cj, hs], cur[cj, 2 * CW + hh * 64:2 * CW + (hh + 1) * 64], R=[Kd, cur], W=[pb3])
                            S.mm(pb3[0:64, CW + hh * 64:CW + (hh + 1) * 64], nBd[cj, hs], U_[cj, hs], R=[nBd, U_], W=[pb3])
                        if BCUT == 4:
                            continue
                        S.tt("dve", st[:, :, :], st[:, :, :], PCc[0:64, 0:2 * HG].rearrange("p (h j) -> p h j", j=2)[:, :, j:j + 1].to_broadcast([64, HG, 64]),
                             ALU.mult, R=[st, PCc], W=[st])
                        S.tt("dve", st[:, :, :], st[:, :, :], pb3[0:64, 0:CW].rearrange("p (h v) -> p h v", v=64), ALU.add, R=[st, pb3], W=[st])
                        S.tt("dve", st[:, :, :], st[:, :, :], pb3[0:64, CW:2 * CW].rearrange("p (h v) -> p h v", v=64), ALU.add, R=[st, pb3], W=[st])
                        yield
                    if RWCUT == 8:
                        continue
                    mean, var = sm(), sm()
                    cen, sq2, bon = g2(), g2(), g2()
                    S.op("dve", lambda e, o=mean, i=otok: e.tensor_reduce(out=o[:, :], in_=v3(i[:, :]), axis=AX.X, op=ALU.add), R=[otok], W=[mean])
                    S.ts("dve", mean[:, :], mean[:, :], 1.0 / 64, None, ALU.mult, R=[mean], W=[mean])
                    S.tt("dve", v3(cen[:, :]), v3(otok[:, :]), mean[:, :].unsqueeze(2).to_broadcast([128, HG, 64]), ALU.subtract, R=[otok, mean], W=[cen])
                    S.act(sq2[:, :], cen[:, :], AF.Square, R=[cen], W=[sq2])
                    S.op("dve", lambda e, o=var, i=sq2: e.tensor_reduce(out=o[:, :], in_=v3(i[:, :]), axis=AX.X, op=ALU.add), R=[sq2], W=[var])
                    S.act(var[:, :], var[:, :], AF.Sqrt, R=[var], W=[var], bias=RW_GN_EPS, scale=1.0 / 64)
                    S.recip(var[:, :], var[:, :], R=[var], W=[var])
                    S.tt("dve", v3(cen[:, :]), v3(cen[:, :]), var[:, :].unsqueeze(2).to_broadcast([128, HG, 64]), ALU.mult, R=[cen, var], W=[cen])
                    S.tt("pool", cen[:, :], cen[:, :], ln_w[:, :], ALU.mult, R=[cen, ln_w], W=[cen])
                    S.tt("pool", cen[:, :], cen[:, :], ln_b[:, :], ALU.add, R=[cen, ln_b], W=[cen])
                    S.tt("dve", v3(bon[:, :]), v3(xv), ss2[:, :].unsqueeze(2).to_broadcast([128, HG, 64]), ALU.mult, R=[cur, ss2], W=[bon])
                    S.tt("pool", cen[:, :], cen[:, :], bon[:, :], ALU.add, R=[cen, bon], W=[cen])
                    S.act(sq2[:, :], xz, AF.Silu, R=[cur], W=[sq2])
                    S.tt("dve", cen[:, :], cen[:, :], sq2[:, :], ALU.mult, R=[cen, sq2], W=[cen])
                    S.dma("sp", ro_d[d][g0:g0 + 128, c0:c0 + CW], cen[:, :], R=[cen], W=[r_ro[d][tt]])
                    yield
                    if RWCUT == 9:
                        continue

            interleave([stream(0), stream(1)], lead=[9, 0])
            S.barrier()
            S.release(mkg)
        ra = [S.sb("ra", [128, 512]) for _ in range(2)]
        rb = [S.sb("rb", [128, 512]) for _ in range(2)]
        yTr = S.sb("yTr", [128, 4, T], BF16)
        for tt in range(NT):
            a_, b_ = ra[tt % 2], rb[tt % 2]
            S.dma("sp", a_[:, :], ro_d[0][tt * 128:(tt + 1) * 128, :], R=[r_ro[0][tt]], W=[a_])
            S.dma("sp", b_[:, :], ro_d[1][tt * 128:(tt + 1) * 128, :], R=[r_ro[1][tt]], W=[b_])
            S.tt("pool", a_[:, :], a_[:, :], b_[:, :], ALU.add, R=[a_, b_], W=[a_])
            pb = self.PB[tt % 4]
            for q in range(4):
                S.tr(pb[:, q * 128:(q + 1) * 128], a_[:, q * 128:(q + 1) * 128], ident, R=[a_, cst], W=[pb])
            S.copy("act" if tt % 2 == 0 else "dve", yTr[:, :, tt * 128:(tt + 1) * 128], pb[:, :].rearrange("p (q t) -> p q t", t=128), R=[pb], W=[yTr])
        S.dma("sp", self.yT[1536:2048, :].rearrange("(q p) t -> p q t", p=128), yTr[:, :, :], R=[yTr], W=self.r_yT[12:16])
        S.barrier()
        S.release(mk)


def make_in_maps(inputs, bld, n_cores):
    f = lambda a: np.ascontiguousarray(np.asarray(a, dtype=np.float32))
    L = bld.DEPTH
    wk = np.arange(64)[:, None]
    wq = np.arange(64)[None, :]
    idx = np.clip(wk - wq + 15, 0, 30)
    rpb = np.asarray(inputs["na_rpb"], np.float32)
    rpb_exp = f(rpb[:, :, :, idx].reshape(L, NA_H, 15 * 64, 64))
    shared = {
        "w_mod": f(inputs["w_mod"]), "b_mod": f(inputs["b_mod"]), "norm_g": f(inputs["norm_g"]),
        "w_in": f(inputs["w_in"]), "w_out": f(inputs["w_out"]),
        "na_qn": f(inputs["na_qn"]), "na_kn": f(inputs["na_kn"]), "rpb_exp": rpb_exp,
        "dn_conv": f(inputs["dn_conv"]), "dn_A_log": f(np.asarray(inputs["dn_A_log"]).reshape(L, 8)),
        "dn_dt_bias": f(np.asarray(inputs["dn_dt_bias"]).reshape(L, 8)), "dn_norm": f(inputs["dn_norm"]),
        "rw_mu": f(inputs["rw_mu"]), "rw_w0": f(inputs["rw_w0"]), "rw_w_up": f(inputs["rw_w_up"]),
        "rw_a0": f(inputs["rw_a0"]), "rw_a_up": f(inputs["rw_a_up"]), "rw_k_k": f(inputs["rw_k_k"]),
        "rw_k_a": f(inputs["rw_k_a"]), "rw_r_k": f(np.asarray(inputs["rw_r_k"]).reshape(L, 512)),
        "rw_ln_w": f(inputs["rw_ln_w"]), "rw_ln_b": f(inputs["rw_ln_b"]),
        "consts": f(bld.consts_np), "rope": f(bld.rope_np),
    }
    x = np.asarray(inputs["x"], np.float32)
    ctx = np.asarray(inputs["ctx"], np.float32)
    c = np.asarray(inputs["c"], np.float32)
    c_ctx = np.asarray(inputs["c_ctx"], np.float32)
    maps = []
    for b in range(n_cores):
        m = dict(shared)
        m["x"] = f(x[b])
        m["ctx"] = f(ctx[b])
        m["cc"] = f(np.stack([c[b], c_ctx]))
        maps.append(m)
    return maps


def kernel(**inputs):
    bld = Builder()
    nc = bld.build()
    maps = make_in_maps(inputs, bld, 8)
    res = run_bass_kernel_spmd(nc, maps, core_ids=list(range(8)))
    return np.stack([np.asarray(r["out"], np.float32) for r in res.results], axis=0)
```
